# Optimizing a Trainium2 kernel written in Bass

```python
import jax, jax.numpy as jnp
from jax import lax
import numpy as np

D_MODEL = 2048
BATCH = 8
SEQ = 2048
DEPTH = 4
DEC_BATCH = 8
DEC_SEQ = 64
PAST_LEN = 1024

CHUNK = 64
N_MIXERS = 2
N_RGLRU = (DEPTH + 1) // 2
N_GLA = DEPTH // 2
EPS = 1e-6
D_RNN = D_MODEL
RG_BLOCKS = 8
RG_BW = D_RNN // RG_BLOCKS
CONV_W = 4
RG_C = 8.0
GLA_HEADS = 4
HEAD_K = (D_MODEL // 2) // GLA_HEADS
HEAD_V = D_MODEL // GLA_HEADS
GATE_RANK = 16
GATE_NORM = 16.0
GLA_DQ = GLA_HEADS * HEAD_K
GLA_DV = GLA_HEADS * HEAD_V
GLA_IN = 2 * GLA_DQ + 2 * GLA_DV + GATE_RANK
D_FF = ((8 * D_MODEL // 3 + 255) // 256) * 256

kernel_name = "hybrid_rglru_gla_streaming_step"


def rms_norm(x, w):
    xf = x.astype(jnp.float32)
    y = xf * lax.rsqrt(jnp.mean(xf * xf, axis=-1, keepdims=True) + EPS)
    return (y * w.astype(jnp.float32)).astype(x.dtype)


def rglru_mixer(x, h0, conv0, w_in, conv_w, conv_b, w_a, b_a, w_x, b_x, lam, w_out):
    B, T, _ = x.shape
    gate, u = jnp.split(x @ w_in, 2, axis=-1)
    upad = jnp.concatenate([conv0.astype(u.dtype), u], axis=1)
    new_conv = upad[:, T:]
    uc = conv_b + sum(upad[:, k:k + T] * conv_w[k] for k in range(CONV_W))
    ub = uc.reshape(B, T, RG_BLOCKS, RG_BW)
    r = jax.nn.sigmoid(jnp.einsum('btnc,ncd->btnd', ub, w_a).reshape(B, T, D_RNN) + b_a)
    i = jax.nn.sigmoid(jnp.einsum('btnc,ncd->btnd', ub, w_x).reshape(B, T, D_RNN) + b_x)
    log_a = RG_C * r.astype(jnp.float32) * jax.nn.log_sigmoid(lam.astype(jnp.float32))
    a = jnp.exp(log_a)
    bterm = jnp.sqrt(-jnp.expm1(2.0 * log_a)) * (i * uc).astype(jnp.float32)
    bterm = bterm.at[:, 0].add(a[:, 0] * h0.astype(jnp.float32))

    def combine(left, right):
        a1, b1 = left
        a2, b2 = right
        return a1 * a2, a2 * b1 + b2

    _, h = lax.associative_scan(combine, (a, bterm), axis=1)
    y = (h.astype(x.dtype) * jax.nn.gelu(gate)) @ w_out
    return y, h[:, -1].astype(x.dtype), new_conv


def gla_mixer(x, S0, w_in, w_gk2, b_gk, g_norm_w, w_out):
    B, T, _ = x.shape
    f32 = jnp.float32
    q, k, v, g, glr = jnp.split(x @ w_in, [GLA_DQ, 2 * GLA_DQ, 2 * GLA_DQ + GLA_DV,
                                           2 * GLA_DQ + 2 * GLA_DV], axis=-1)
    gk = jax.nn.log_sigmoid((glr @ w_gk2 + b_gk).astype(f32)) / GATE_NORM
    C = CHUNK if T % CHUNK == 0 else T
    N = T // C
    qh = q.astype(f32).reshape(B, N, C, GLA_HEADS, HEAD_K) * (HEAD_K ** -0.5)
    kh = k.astype(f32).reshape(B, N, C, GLA_HEADS, HEAD_K)
    vh = v.astype(f32).reshape(B, N, C, GLA_HEADS, HEAD_V)
    bcum = jnp.cumsum(gk.reshape(B, N, C, GLA_HEADS, HEAD_K), axis=2)
    blast = bcum[:, :, -1]
    qe = qh * jnp.exp(bcum)
    ke = kh * jnp.exp(-bcum)
    kd = kh * jnp.exp(blast[:, :, None] - bcum)
    mask = jnp.tril(jnp.ones((C, C), dtype=bool))
    att = jnp.where(mask, jnp.einsum('bnchd,bnshd->bnhcs', qe, ke), 0.0)
    o_intra = jnp.einsum('bnhcs,bnshv->bnchv', att, vh)

    def step(S, inp):
        qe_n, kd_n, v_n, bl_n = inp
        o_n = jnp.einsum('bchd,bhdv->bchv', qe_n, S)
        S = jnp.exp(bl_n)[..., None] * S + jnp.einsum('bchd,bchv->bhdv', kd_n, v_n)
        return S, o_n

    xs = (jnp.moveaxis(qe, 1, 0), jnp.moveaxis(kd, 1, 0), jnp.moveaxis(vh, 1, 0), jnp.moveaxis(blast, 1, 0))
    S_fin, o_inter = lax.scan(step, S0.astype(f32), xs)
    o = (o_intra + jnp.moveaxis(o_inter, 0, 1)).reshape(B, T, GLA_HEADS, HEAD_V)
    o = o * lax.rsqrt(jnp.mean(o * o, axis=-1, keepdims=True) + EPS) * g_norm_w.astype(f32)
    o = o.reshape(B, T, GLA_DV) * jax.nn.silu(g.astype(f32))
    y = o.astype(x.dtype) @ w_out
    return y, S_fin.astype(x.dtype)


def trunk(x, h_all, conv_all, S_all, norm_mix, norm_ffn, norm_final,
          rg_w_in, rg_conv_w, rg_conv_b, rg_w_a, rg_b_a, rg_w_x, rg_b_x, rg_lambda, rg_w_out,
          gla_w_in, gla_w_gk2, gla_b_gk, gla_norm_w, gla_w_out, ffn_w_up, ffn_w_down):
    hs, cs, Ss = [], [], []
    for layer in range(DEPTH):
        j = layer // N_MIXERS
        hn = rms_norm(x, norm_mix[layer])
        if layer % N_MIXERS == 0:
            y, h_new, c_new = rglru_mixer(hn, h_all[j], conv_all[j], rg_w_in[j], rg_conv_w[j], rg_conv_b[j],
                                          rg_w_a[j], rg_b_a[j], rg_w_x[j], rg_b_x[j], rg_lambda[j], rg_w_out[j])
            hs.append(h_new)
            cs.append(c_new)
        else:
            y, S_new = gla_mixer(hn, S_all[j], gla_w_in[j], gla_w_gk2[j], gla_b_gk[j], gla_norm_w[j], gla_w_out[j])
            Ss.append(S_new)
        x = x + y
        hn = rms_norm(x, norm_ffn[layer])
        gt, up = jnp.split(hn @ ffn_w_up[layer], 2, axis=-1)
        x = x + (jax.nn.silu(gt) * up) @ ffn_w_down[layer]
    return rms_norm(x, norm_final), jnp.stack(hs), jnp.stack(cs), jnp.stack(Ss)


def setup_inputs(seed: int = 0) -> dict:
    key = jax.random.key(seed)
    ks = jax.random.split(key, 24)
    nrm = jax.random.normal
    f = jnp.float32
    u = jax.random.uniform(ks[10], (N_RGLRU, D_RNN), f, 0.81, 0.998)
    s = u ** (1.0 / RG_C)
    return {
        "x_prompt": nrm(ks[0], (BATCH, SEQ, D_MODEL), f),
        "x_sample": nrm(ks[1], (DEC_BATCH, DEC_SEQ, D_MODEL), f),
        "state_rglru_h": 0.5 * nrm(ks[2], (N_RGLRU, DEC_BATCH, D_RNN), f),
        "state_rglru_conv": nrm(ks[3], (N_RGLRU, DEC_BATCH, CONV_W - 1, D_RNN), f),
        "state_gla": nrm(ks[4], (N_GLA, DEC_BATCH, GLA_HEADS, HEAD_K, HEAD_V), f),
        "norm_mix": 1.0 + 0.05 * nrm(ks[5], (DEPTH, D_MODEL), f),
        "norm_ffn": 1.0 + 0.05 * nrm(ks[6], (DEPTH, D_MODEL), f),
        "norm_final": 1.0 + 0.05 * nrm(ks[7], (D_MODEL,), f),
        "rg_w_in": nrm(ks[8], (N_RGLRU, D_MODEL, 2 * D_RNN), f) * D_MODEL ** -0.5,
        "rg_conv_w": nrm(ks[9], (N_RGLRU, CONV_W, D_RNN), f) * CONV_W ** -0.5,
        "rg_conv_b": 0.01 * nrm(ks[11], (N_RGLRU, D_RNN), f),
        "rg_w_a": nrm(ks[12], (N_RGLRU, RG_BLOCKS, RG_BW, RG_BW), f) * RG_BW ** -0.5,
        "rg_b_a": 0.1 * nrm(ks[13], (N_RGLRU, D_RNN), f),
        "rg_w_x": nrm(ks[14], (N_RGLRU, RG_BLOCKS, RG_BW, RG_BW), f) * RG_BW ** -0.5,
        "rg_b_x": 0.1 * nrm(ks[15], (N_RGLRU, D_RNN), f),
        "rg_lambda": jnp.log(s) - jnp.log1p(-s),
        "rg_w_out": nrm(ks[16], (N_RGLRU, D_RNN, D_MODEL), f) * D_RNN ** -0.5,
        "gla_w_in": nrm(ks[17], (N_GLA, D_MODEL, GLA_IN), f) * D_MODEL ** -0.5,
        "gla_w_gk2": nrm(ks[18], (N_GLA, GATE_RANK, GLA_DQ), f) * GATE_RANK ** -0.5,
        "gla_b_gk": 0.1 * nrm(ks[19], (N_GLA, GLA_DQ), f),
        "gla_norm_w": 1.0 + 0.05 * nrm(ks[20], (N_GLA, HEAD_V), f),
        "gla_w_out": nrm(ks[21], (N_GLA, GLA_DV, D_MODEL), f) * GLA_DV ** -0.5,
        "ffn_w_up": nrm(ks[22], (DEPTH, D_MODEL, 2 * D_FF), f) * D_MODEL ** -0.5,
        "ffn_w_down": nrm(ks[23], (DEPTH, D_FF, D_MODEL), f) * D_FF ** -0.5,
    }


def reference(x_prompt, x_sample, state_rglru_h, state_rglru_conv, state_gla,
              norm_mix, norm_ffn, norm_final,
              rg_w_in, rg_conv_w, rg_conv_b, rg_w_a, rg_b_a, rg_w_x, rg_b_x, rg_lambda, rg_w_out,
              gla_w_in, gla_w_gk2, gla_b_gk, gla_norm_w, gla_w_out, ffn_w_up, ffn_w_down):
    Bp = x_prompt.shape[0]
    dt = x_prompt.dtype
    h0 = jnp.zeros((N_RGLRU, Bp, D_RNN), dt)
    c0 = jnp.zeros((N_RGLRU, Bp, CONV_W - 1, D_RNN), dt)
    S0 = jnp.zeros((N_GLA, Bp, GLA_HEADS, HEAD_K, HEAD_V), dt)
    y_prompt, h_p, c_p, S_p = trunk(x_prompt, h0, c0, S0, norm_mix, norm_ffn, norm_final,
                                    rg_w_in, rg_conv_w, rg_conv_b, rg_w_a, rg_b_a, rg_w_x, rg_b_x, rg_lambda, rg_w_out,
                                    gla_w_in, gla_w_gk2, gla_b_gk, gla_norm_w, gla_w_out, ffn_w_up, ffn_w_down)
    y_sample, h_s, c_s, S_s = trunk(x_sample, state_rglru_h, state_rglru_conv, state_gla, norm_mix, norm_ffn, norm_final,
                                    rg_w_in, rg_conv_w, rg_conv_b, rg_w_a, rg_b_a, rg_w_x, rg_b_x, rg_lambda, rg_w_out,
                                    gla_w_in, gla_w_gk2, gla_b_gk, gla_norm_w, gla_w_out, ffn_w_up, ffn_w_down)
    return (y_prompt, y_sample, h_p, c_p, S_p, h_s, c_s, S_s)
```

```python
import numpy as np
import concourse.bass as bass
import concourse.mybir as mybir
from concourse.bass_utils import run_bass_kernel_spmd

F32 = mybir.dt.float32
BF16 = mybir.dt.bfloat16
AF = mybir.ActivationFunctionType
ALU = mybir.AluOpType
EPS = 1e-6


class Cfg:
    def __init__(s, D=2048, SEQ=2048, DSEQ=64, DEPTH=4, passes=None, nw=3):
        s.D, s.SEQ, s.DSEQ, s.DEPTH = D, SEQ, DSEQ, DEPTH
        s.KC = D // 128
        s.NRG = (DEPTH + 1) // 2
        s.NGLA = DEPTH // 2
        s.RGB = D // 256
        s.H = D // 512
        s.DQ = s.H * 256
        s.QC = s.DQ // 128
        s.GIN = 2 * s.DQ + 2 * D + 16
        s.DFF = ((8 * D // 3 + 255) // 256) * 256
        s.FC = s.DFF // 128
        s.NW = nw
        if passes is None:
            passes = []
            t = 0
            first = True
            while t < SEQ:
                L = min(512, SEQ - t)
                segs = [(0, t, L, L)]
                if first:
                    segs.append((1, 0, 128, DSEQ))
                    first = False
                passes.append(segs)
                t += L
        s.passes = passes
        s.TT = max(sum(g[2] for g in p) for p in passes)
        o = {}
        c = 0
        for name, n in [("nm", DEPTH * s.KC), ("nf", DEPTH * s.KC), ("nfin", s.KC),
                        ("cw", s.NRG * 4 * s.KC), ("cb", s.NRG * s.KC), ("ba", s.NRG * s.KC),
                        ("bx", s.NRG * s.KC), ("lam", s.NRG * s.KC), ("bgk", max(1, s.NGLA) * s.QC),
                        ("gnw", max(1, s.NGLA) * 4), ("h0", s.NRG * s.KC), ("c0", s.NRG * s.KC * 3)]:
            o[name] = c
            c += n
        s.vo = o
        s.NV = c


class Prog:
    ENG = ["pe", "act", "dve", "pool", "sp"]

    def __init__(self, nc):
        self.nc = nc
        self.lists = {e: [] for e in self.ENG}
        self.sems = {}
        self.cnt = {e: 0 for e in self.ENG}
        self.known = {e: {} for e in self.ENG}
        self.res = {}
        self.dmacnt = {}
        self.nps = 0
        self.rec = None

    def sem(self, name):
        if name not in self.sems:
            self.sems[name] = self.nc.alloc_semaphore(name)
        return self.sems[name]

    def _deps(self, reads, writes):
        evs = []
        for k in reads:
            r = self.res.get(k)
            if r and r[0]:
                evs.append(r[0])
        for k in writes:
            r = self.res.get(k)
            if r:
                if r[0]:
                    evs.append(r[0])
                for sn, (v, se) in r[1].items():
                    evs.append((sn, v, se))
        return evs

    def _waits(self, eng, evs):
        need = {}
        for (sn, v, se) in evs:
            if se == eng and eng == "pe":
                continue
            if v > self.known[eng].get(sn, 0) and v > need.get(sn, 0):
                need[sn] = v
        for sn, v in need.items():
            self.known[eng][sn] = v
            sem = self.sem(sn)
            self.lists[eng].append(lambda e, sem=sem, v=v: e.wait_ge(sem, v))

    def _record(self, ev, reads, writes):
        for k in reads:
            r = self.res.setdefault(k, [None, {}])
            old = r[1].get(ev[0])
            if old is None or old[0] < ev[1]:
                r[1][ev[0]] = (ev[1], ev[2])
        for k in writes:
            self.res[k] = [ev, {}]

    def record(self):
        self.rec = []
        return self.rec

    def stop(self):
        r, self.rec = self.rec, None
        return r

    def interleave(self, lists):
        k = 0
        more = True
        while more:
            more = False
            for l in lists:
                if k < len(l):
                    more = True
                    it = l[k]
                    if it[0] == "op":
                        self.op(*it[1:])
                    else:
                        self.dma(*it[1:])
            k += 1

    def op(self, eng, fn, reads=(), writes=()):
        if self.rec is not None:
            self.rec.append(("op", eng, fn, tuple(reads), tuple(writes)))
            return
        self._waits(eng, self._deps(reads, writes))
        self.cnt[eng] += 1
        sn = "S_" + eng
        sem = self.sem(sn)
        self.lists[eng].append(lambda e, fn=fn, sem=sem: fn(e).then_inc(sem, 1))
        self._record((sn, self.cnt[eng], eng), reads, writes)

    def dma(self, q, pairs, reads, writes, semname):
        if self.rec is not None:
            self.rec.append(("dma", q, pairs, tuple(reads), tuple(writes), semname))
            return
        self._waits(q, self._deps(reads, writes))
        sem = self.sem(semname)
        for (o, i) in pairs:
            self.dmacnt[semname] = self.dmacnt.get(semname, 0) + 1
            self.lists[q].append(lambda e, o=o, i=i, sem=sem: e.dma_start(out=o, in_=i).then_inc(sem, 16))
        self._record((semname, 16 * self.dmacnt[semname], "dma"), reads, writes)

    def barrier(self, engs=("pe", "act", "dve", "sp")):
        evs = []
        for e in ("pe", "act", "dve", "pool"):
            if self.cnt[e]:
                evs.append(("S_" + e, self.cnt[e], e))
        for sn, n in self.dmacnt.items():
            if not sn.startswith("W"):
                evs.append((sn, 16 * n, "dma"))
        for e in engs:
            self._waits(e, [ev for ev in evs if not (ev[2] == e and e == "pe")])

    def final_wait(self, eng="sp"):
        evs = [(sn, 16 * n, "dma") for sn, n in self.dmacnt.items()]
        for e in ("pe", "act", "dve", "pool"):
            if self.cnt[e]:
                evs.append(("S_" + e, self.cnt[e], e))
        self._waits(eng, evs)

    def emit(self):
        nc = self.nc
        with nc.Block() as block:
            for name, attr in [("sp", "sync"), ("pool", "gpsimd"), ("pe", "tensor"),
                               ("act", "scalar"), ("dve", "vector")]:
                lst = self.lists[name]
                if not lst:
                    continue

                def body(e, lst=lst):
                    for f in lst:
                        f(e)
                getattr(block, attr)(body)


class Arena:
    def __init__(self, nc, base, size):
        self.nc, self.base, self.size, self.top = nc, base, size, 0
        self.n = 0
        self.peak = 0

    def alloc(self, name, shape, dtype):
        esz = 2 if dtype == BF16 else 4
        nb = esz
        for d in shape[1:]:
            nb *= d
        nb = (nb + 31) // 32 * 32
        assert self.top + nb <= self.size, f"arena overflow {name}: {self.top}+{nb}>{self.size}"
        self.n += 1
        t = self.nc.alloc_sbuf_tensor_at(f"{name}_{self.n}", list(shape), dtype, offset=self.base + self.top)
        self.top += nb
        self.peak = max(self.peak, self.top)
        return t

    def mark(self):
        return self.top

    def release(self, m):
        self.top = m


def build(cfg):
    nc = bass.Bass("TRN2", target_bir_lowering=False)
    P = Prog(nc)
    D, KC, TT = cfg.D, cfg.KC, cfg.TT
    NRG, NGLA, H, DQ, QC, FC, DFF = cfg.NRG, cfg.NGLA, cfg.H, cfg.DQ, cfg.QC, cfg.FC, cfg.DFF
    vo = cfg.vo

    def din(name, shape):
        return nc.dram_tensor(name, list(shape), F32, kind="ExternalInput").ap()

    def dout(name, shape):
        return nc.dram_tensor(name, list(shape), F32, kind="ExternalOutput").ap()

    xin = [din("xp", [D, cfg.SEQ]), din("xs", [D, cfg.DSEQ])]
    vecs_d = din("vecs", [128, cfg.NV])
    sgla_d = din("sgla", [max(1, NGLA), H, 256, 512])
    w_rg_in = din("rg_w_in", [NRG, D, 2 * D])
    w_rg_a = din("rg_w_a", [NRG, cfg.RGB, 256, 256])
    w_rg_x = din("rg_w_x", [NRG, cfg.RGB, 256, 256])
    w_rg_out = din("rg_w_out", [NRG, D, D])
    w_gla_in = din("gla_w_in", [max(1, NGLA), D, cfg.GIN])
    w_gla_gk2 = din("gla_w_gk2", [max(1, NGLA), 16, DQ])
    w_gla_out = din("gla_w_out", [max(1, NGLA), D, D])
    w_up = din("ffn_w_up", [cfg.DEPTH, D, 2 * DFF])
    w_dn = din("ffn_w_down", [cfg.DEPTH, DFF, D])
    yout = [dout("yp", [D, cfg.SEQ]), dout("ys", [D, cfg.DSEQ])]
    o_h = dout("o_h", [128, 2 * NRG * KC])
    o_conv = dout("o_conv", [128, 2 * NRG * KC * 3])
    o_gla = [dout("o_gla_p", [max(1, NGLA), H, 256, 512]), dout("o_gla_s", [max(1, NGLA), H, 256, 512])]


    arena_bytes = 207 * 1024
    nc.alloc_sbuf_tensor("arena", [128, arena_bytes // 4], F32)
    A = Arena(nc, nc.lookup_mloc("arena").addr, arena_bytes)
    NGm = max(1, NGLA)
    vecs = A.alloc("vecs", [128, cfg.NV], F32)
    cl = A.alloc("cl", [128, NRG * KC], F32)
    cl2 = A.alloc("cl2", [128, NRG * KC], F32)
    nbgk = A.alloc("nbgk", [128, NGm * QC], F32)
    clh = A.alloc("clh", [128, NRG * KC], F32)
    hb = A.alloc("hb", [128, 2 * NRG * KC], F32)
    hst = A.alloc("hst", [128, 2 * NRG * KC], F32)
    cst = A.alloc("cst", [128, 2 * NRG * KC * 3], F32)
    ident = A.alloc("ident", [128, 128], BF16)
    zer = A.alloc("zer", [128, 128], BF16)
    ones = A.alloc("ones", [128, 128], BF16)
    onesf = A.alloc("onesf", [128, 128], F32)
    mask = A.alloc("mask", [128, 128], F32)
    rmA = A.alloc("rmA", [128, 1], F32)
    rmB = A.alloc("rmB", [128, 1], F32)
    rmask = A.alloc("rmask", [128, 512], F32)
    rs = A.alloc("rs", [128, 512], F32)
    rstd = A.alloc("rstd", [128, 512], F32)
    xT = A.alloc("xT", [128, KC, TT], F32)
    hn = A.alloc("hn", [128, KC, TT], BF16)
    mix_off = A.base + A.top
    mix = A.alloc("mix", [128, KC, TT], BF16)
    sqb = nc.alloc_sbuf_tensor_at("sqb_alias", [128, KC * TT], BF16, offset=mix_off)
    WSL = 8192
    wgk = A.alloc("wgk", [16, DQ], BF16)
    wr = [A.alloc(f"wr{i}", [128, WSL], BF16) for i in range(cfg.NW)]
    ps = [nc.alloc_psum_tensor(f"ps{i}", [128, 512], F32) for i in range(8)]
    psb = [p.bitcast(BF16) for p in ps]
    st = {"wn": 0, "ps": 0, "po": 0}

    def newps():
        i = st["ps"] % 6
        st["ps"] += 1
        return i

    def newpo():
        i = 6 + st["po"] % 2
        st["po"] += 1
        return i

    def wload(pairs_fn):
        i = st["wn"] % cfg.NW
        st["wn"] += 1
        P.dma("pool", pairs_fn(wr[i]), reads=[], writes=[("w", i)], semname=f"W{i}")
        return wr[i], ("w", i)

    def wview(t, kcn, w):
        return t[:, 0:kcn * w].rearrange("p (k n) -> p k n", n=w)

    def rows(w2d):
        return w2d.rearrange("(k p) n -> p k n", p=128)

    def vcol(name, i):
        return vecs[:, vo[name] + i: vo[name] + i + 1]

    P.dma("sp", [(vecs[:, :], vecs_d[:, :])], reads=[], writes=["vecs"], semname="Lv")
    n1 = NRG * KC
    P.op("dve", lambda e: e.memset(hst[:, 0:n1], 0.0), writes=["hst"])
    P.op("dve", lambda e: e.tensor_copy(out=hst[:, n1:2 * n1], in_=vecs[:, vo["h0"]:vo["h0"] + n1]),
         reads=["vecs"], writes=["hst"])
    P.op("dve", lambda e: e.memset(cst[:, 0:3 * n1], 0.0), writes=["cst"])
    P.op("dve", lambda e: e.tensor_copy(out=cst[:, 3 * n1:6 * n1], in_=vecs[:, vo["c0"]:vo["c0"] + 3 * n1]),
         reads=["vecs"], writes=["cst"])
    P.op("act", lambda e: e.activation(out=cl[:, :], in_=vecs[:, vo["lam"]:vo["lam"] + n1], func=AF.Exp, scale=-1.0),
         reads=["vecs"], writes=["cl"])
    P.op("act", lambda e: e.activation(out=cl[:, :], in_=cl[:, :], func=AF.Ln, bias=1.0), reads=["cl"], writes=["cl"])
    P.op("dve", lambda e: e.tensor_scalar(out=cl2[:, :], in0=cl[:, :], scalar1=-16.0, scalar2=None, op0=ALU.mult),
         reads=["cl"], writes=["cl2"])
    P.op("dve", lambda e: e.tensor_scalar(out=cl[:, :], in0=cl[:, :], scalar1=-8.0, scalar2=None, op0=ALU.mult),
         reads=["cl", "cl2"], writes=["cl"])
    P.op("dve", lambda e: e.tensor_scalar(out=clh[:, :], in0=cl[:, :], scalar1=0.5, scalar2=None, op0=ALU.mult),
         reads=["cl"], writes=["clh"])
    P.op("dve", lambda e: e.tensor_scalar(out=hb[:, 0:n1], in0=vecs[:, vo["ba"]:vo["ba"] + n1], scalar1=0.5, scalar2=None, op0=ALU.mult),
         reads=["vecs"], writes=["hb"])
    P.op("dve", lambda e: e.tensor_scalar(out=hb[:, n1:2 * n1], in0=vecs[:, vo["bx"]:vo["bx"] + n1], scalar1=0.5, scalar2=None, op0=ALU.mult),
         reads=["vecs", "hb"], writes=["hb"])
    P.op("dve", lambda e: e.tensor_scalar(out=nbgk[:, :], in0=vecs[:, vo["bgk"]:vo["bgk"] + NGm * QC], scalar1=-1.0,
                                          scalar2=None, op0=ALU.mult), reads=["vecs"], writes=["nbgk"])
    P.op("dve", lambda e: e.memset(zer[:, :], 0.0), writes=["zer"])
    P.op("dve", lambda e: e.memset(ones[:, :], 1.0), writes=["ones"])
    P.op("dve", lambda e: e.memset(onesf[:, :], 1.0), writes=["onesf"])
    P.op("pool", lambda e: e.affine_select(out=ident[:, :], in_=zer[:, :], pattern=[[1, 128]], compare_op=ALU.not_equal,
                                           fill=1.0, base=0, channel_multiplier=-1), reads=["zer"], writes=["ident"])
    P.op("pool", lambda e: e.affine_select(out=mask[:, :], in_=onesf[:, :], pattern=[[1, 128]], compare_op=ALU.is_ge,
                                           fill=0.0, base=0, channel_multiplier=-1), reads=["onesf"], writes=["mask"])
    P.op("dve", lambda e: e.memset(mask[0:64, 64:128], 0.0), reads=["mask"], writes=["mask"])
    P.op("dve", lambda e: e.memset(rmA[0:64, :], 1.0), writes=["rmA"])
    P.op("dve", lambda e: e.memset(rmA[64:128, :], 0.0), reads=["rmA"], writes=["rmA"])
    P.op("dve", lambda e: e.memset(rmB[0:64, :], 0.0), writes=["rmB"])
    P.op("dve", lambda e: e.memset(rmB[64:128, :], 1.0), reads=["rmB"], writes=["rmB"])
    P.op("dve", lambda e: e.memset(rmask[:, :], 1.0), writes=["rmask"])
    P.op("dve", lambda e: e.memset(rmask[:, :].rearrange("p (n c) -> p n c", c=64)[:, :, 0:1], 0.0),
         reads=["rmask"], writes=["rmask"])

    P.barrier(("pe", "act", "dve", "sp", "pool"))

    def mm_group(pi, L, kcn, lhs_fn, rhs_fn, reads, col0=0):
        def f(e):
            ins = None
            for k in range(kcn):
                ins = e.matmul(ps[pi][:, col0:col0 + L], lhsT=lhs_fn(k), rhs=rhs_fn(k), start=(k == 0), stop=(k == kcn - 1))
            return ins
        P.op("pe", f, reads=reads, writes=[("ps", pi)])

    def rmsnorm(segs, wname, wbase, out_t, out_key, sqb):
        KH = KC // 2
        for si, sg in enumerate(segs):
            off, L = sg["off"], sg["L"]
            xk = [("x", si, m) for m in range(KC)]
            sqv = sqb[:, 0:KC * L].rearrange("p (k t) -> p k t", t=L)
            mixk = [("mix", s2) for s2 in range(len(segs))] + [("mix", s2, c) for s2 in range(len(segs)) for c in range(KC)]
            P.op("act", lambda e, sqv=sqv, off=off, L=L: e.activation(out=sqv[:, 0:KH, :], in_=xT[:, 0:KH, off:off + L], func=AF.Square),
                 reads=xk[0:KH], writes=["sqbA"] + mixk)
            P.op("dve", lambda e, sqv=sqv, off=off, L=L: e.tensor_tensor(out=sqv[:, KH:KC, :], in0=xT[:, KH:KC, off:off + L],
                                                                         in1=xT[:, KH:KC, off:off + L], op=ALU.mult),
                 reads=xk[KH:KC], writes=["sqbB"] + mixk)
            pi = newps()
            mm_group(pi, L, KC, lambda k: ones[:, :], lambda k, sqv=sqv: sqv[:, k, :], reads=["sqbA", "sqbB", "ones"] + mixk)
            P.op("act", lambda e, pi=pi, L=L: e.activation(out=rs[:, 0:L], in_=ps[pi][:, 0:L], func=AF.Ln, bias=EPS, scale=1.0 / D),
                 reads=[("ps", pi)], writes=["rs"])
            P.op("act", lambda e, L=L: e.activation(out=rstd[:, 0:L], in_=rs[:, 0:L], func=AF.Exp, scale=-0.5), reads=["rs"], writes=["rstd"])
            for k in range(KC):
                P.op("dve", lambda e, k=k, off=off, L=L: e.scalar_tensor_tensor(
                    out=out_t[:, k, off:off + L], in0=xT[:, k, off:off + L], scalar=vcol(wname, wbase + k),
                    in1=rstd[:, 0:L], op0=ALU.mult, op1=ALU.mult),
                    reads=[("x", si, k), "rstd", "vecs"] + (["sqbA", "sqbB"] if out_t is xT else []), writes=[out_key(si, k)])

    def add_into_x(si, m, off, L, pi):
        P.op("dve", lambda e: e.tensor_tensor(out=xT[:, m, off:off + L], in0=xT[:, m, off:off + L], in1=ps[pi][:, 0:L],
                                              op=ALU.add), reads=[("x", si, m), ("ps", pi)], writes=[("x", si, m)])

    def out_proj(segs, w2d, mixkeys=None):
        if mixkeys is None:
            mixkeys = [[("mix", si)] for si in range(len(segs))]
        for n0 in range(0, D, 512):
            wt, wk = wload(lambda t, n0=n0: [(wview(t, KC, 512), rows(w2d)[:, :, n0:n0 + 512])])
            wv = wview(wt, KC, 512)
            for m in range(4):
                for si, sg in enumerate(segs):
                    off, L = sg["off"], sg["L"]
                    pi = newps()
                    mm_group(pi, L, KC, lambda k, m=m, wv=wv: wv[:, k, m * 128:(m + 1) * 128],
                             lambda k, off=off, L=L: mix[:, k, off:off + L], reads=[wk] + mixkeys[si])
                    add_into_x(si, n0 // 128 + m, off, L, pi)

    def ffn(segs, l):
        m0 = A.mark()
        hT = [A.alloc("hT", [128, 2, TT], BF16) for _ in range(2)]
        stmp = [A.alloc("stmp", [128, 512], F32) for _ in range(2)]
        sn = 0
        for blk in range(FC // 2):
            c0 = blk * 256
            wt, wk = wload(lambda t, c0=c0: [(wview(t, KC, 512)[:, :, 0:256], rows(w_up[l])[:, :, c0:c0 + 256]),
                                              (wview(t, KC, 512)[:, :, 256:512], rows(w_up[l])[:, :, DFF + c0:DFF + c0 + 256])])
            wv = wview(wt, KC, 512)
            hb = blk % 2
            for j in range(2):
                for si, sg in enumerate(segs):
                    off, L = sg["off"], sg["L"]
                    pg, pu = newps(), newps()
                    mm_group(pg, L, KC, lambda k, j=j, wv=wv: wv[:, k, j * 128:(j + 1) * 128],
                             lambda k, off=off, L=L: hn[:, k, off:off + L], reads=[wk, ("hn", si)])
                    mm_group(pu, L, KC, lambda k, j=j, wv=wv: wv[:, k, 256 + j * 128:256 + (j + 1) * 128],
                             lambda k, off=off, L=L: hn[:, k, off:off + L], reads=[wk, ("hn", si)])
                    tb = sn % 2
                    sn += 1
                    P.op("act", lambda e, tb=tb, pg=pg, L=L: e.activation(out=stmp[tb][:, 0:L], in_=ps[pg][:, 0:L], func=AF.Silu),
                         reads=[("ps", pg)], writes=[("stmp", tb)])
                    P.op("dve", lambda e, tb=tb, pu=pu, L=L, off=off, hb=hb, j=j: e.tensor_tensor(
                        out=hT[hb][:, j, off:off + L], in0=stmp[tb][:, 0:L], in1=ps[pu][:, 0:L], op=ALU.mult),
                        reads=[("stmp", tb), ("ps", pu)], writes=[("hT", hb, si)])
            wt2, wk2 = wload(lambda t, c0=c0: [(wview(t, 2, D), w_dn[l][c0:c0 + 256, :].rearrange("(j p) n -> p j n", p=128))])
            wv2 = wview(wt2, 2, D)
            for m in range(KC):
                for si, sg in enumerate(segs):
                    off, L = sg["off"], sg["L"]
                    pi = newps()
                    mm_group(pi, L, 2, lambda j, m=m, wv2=wv2: wv2[:, j, m * 128:(m + 1) * 128],
                             lambda j, off=off, L=L, hb=hb: hT[hb][:, j, off:off + L], reads=[wk2, ("hT", hb, si)])
                    add_into_x(si, m, off, L, pi)
        P.barrier()
        A.release(m0)

    def rglru(segs, j):
        m0 = A.mark()
        NS = len(segs)
        nslot = 2 * NS
        sL = [segs[r % NS]["L"] for r in range(nslot)]
        ubuf = [A.alloc("ubuf", [128, 2, sL[r] + 4], F32) for r in range(nslot)]
        uc = [A.alloc("uc", [128, 2, sL[r]], F32) for r in range(nslot)]
        ucb = [A.alloc("ucb", [128, 2, sL[r]], BF16) for r in range(nslot)]
        ggb = [A.alloc("ggb", [128, 2, sL[r]], BF16) for r in range(nslot)]
        TB = [[[A.alloc("rgTB", [128, segs[si]["L"]], F32) for _ in range(2)] for jo in range(2)] for si in range(NS)]

        def loadA(b):
            c0 = b * 256
            wt, wk = wload(lambda t, c0=c0: [(wview(t, KC, 512)[:, :, 0:256], rows(w_rg_in[j])[:, :, c0:c0 + 256]),
                                              (wview(t, KC, 512)[:, :, 256:512], rows(w_rg_in[j])[:, :, D + c0:D + c0 + 256])])
            return wview(wt, KC, 512), wk

        def stageA(b, si, wv, wk):
            lists = []
            sg = segs[si]
            if True:
                off, L, Lr, seq = sg["off"], sg["L"], sg["Lr"], sg["seq"]
                r = (b % 2) * NS + si
                kub, kuc, kucb, kgg = ("ubuf", m0, r), ("uc", m0, r), ("ucb", m0, r), ("ggb", m0, r)
                for jj in range(2):
                    P.record()
                    c = 2 * b + jj
                    sidx = (seq * NRG + j) * KC + c
                    pg, pu = 2 * jj, 2 * jj + 1
                    mm_group(pg, L, KC, lambda k, jj=jj, wv=wv: wv[:, k, jj * 128:(jj + 1) * 128],
                             lambda k, off=off, L=L: hn[:, k, off:off + L], reads=[wk, ("hn", si)])
                    mm_group(pu, L, KC, lambda k, jj=jj, wv=wv: wv[:, k, 256 + jj * 128:256 + (jj + 1) * 128],
                             lambda k, off=off, L=L: hn[:, k, off:off + L], reads=[wk, ("hn", si)])
                    P.op("act", lambda e, r=r, jj=jj, pg=pg, L=L: e.activation(out=ggb[r][:, jj, 0:L], in_=ps[pg][:, 0:L],
                                                                                func=AF.Gelu_apprx_tanh),
                         reads=[("ps", pg)], writes=[(kgg, jj)])
                    P.op("act", lambda e, r=r, jj=jj, pu=pu, L=L: e.activation(out=ubuf[r][:, jj, 3:3 + L], in_=ps[pu][:, 0:L],
                                                                                func=AF.Copy),
                         reads=[("ps", pu)], writes=[(kub, jj, "b")])
                    P.op("dve", lambda e, r=r, jj=jj, sidx=sidx: e.tensor_copy(out=ubuf[r][:, jj, 0:3],
                                                                             in_=cst[:, 3 * sidx:3 * sidx + 3]),
                         reads=[("cst", sidx)], writes=[(kub, jj, "a")])
                    ur = [(kub, jj, "a"), (kub, jj, "b")]
                    cwi = lambda kk, c=c: vcol("cw", (j * 4 + kk) * KC + c)
                    P.op("dve", lambda e, r=r, jj=jj, L=L, c=c, cwi=cwi: e.tensor_scalar(
                        out=uc[r][:, jj, 0:L], in0=ubuf[r][:, jj, 0:L], scalar1=cwi(0), scalar2=vcol("cb", j * KC + c),
                        op0=ALU.mult, op1=ALU.add), reads=ur + ["vecs"], writes=[(kuc, jj)])
                    for kk in range(1, 4):
                        P.op("dve", lambda e, r=r, jj=jj, L=L, kk=kk, cwi=cwi: e.scalar_tensor_tensor(
                            out=uc[r][:, jj, 0:L], in0=ubuf[r][:, jj, kk:kk + L], scalar=cwi(kk), in1=uc[r][:, jj, 0:L],
                            op0=ALU.mult, op1=ALU.add), reads=ur + [(kuc, jj), "vecs"], writes=[(kuc, jj)])
                    P.op("dve", lambda e, r=r, jj=jj, Lr=Lr, sidx=sidx: e.tensor_copy(out=cst[:, 3 * sidx:3 * sidx + 3],
                                                                                     in_=ubuf[r][:, jj, Lr:Lr + 3]),
                         reads=ur, writes=[("cst", sidx)])
                    P.op("act", lambda e, r=r, jj=jj, L=L: e.activation(out=ucb[r][:, jj, 0:L], in_=uc[r][:, jj, 0:L], func=AF.Copy),
                         reads=[(kuc, jj)], writes=[(kucb, jj)])
                    lists.append(P.stop())
            return lists

        def loadB(b):
            wt2, wk2 = wload(lambda t, b=b: [(wview(t, 2, 256), w_rg_a[j, b].rearrange("(c p) n -> p c n", p=128)),
                                              (t[:, 512:1024].rearrange("p (c n) -> p c n", n=256),
                                               w_rg_x[j, b].rearrange("(c p) n -> p c n", p=128))])
            wa = wview(wt2, 2, 256)
            wx = wt2[:, 512:1024].rearrange("p (c n) -> p c n", n=256)
            return wa, wx, wk2

        def stageB(b, si, wa, wx, wk2):
            lists = []
            sg = segs[si]
            if True:
                off, L, Lr, seq = sg["off"], sg["L"], sg["Lr"], sg["seq"]
                r = (b % 2) * NS + si
                kub, kuc, kucb, kgg = ("ubuf", m0, r), ("uc", m0, r), ("ucb", m0, r), ("ggb", m0, r)
                for jo in range(2):
                    P.record()
                    c = 2 * b + jo
                    sidx = (seq * NRG + j) * KC + c
                    pa, px = 4 + 2 * jo, 5 + 2 * jo
                    mm_group(pa, L, 2, lambda ji, jo=jo, wa=wa: wa[:, ji, jo * 128:(jo + 1) * 128],
                             lambda ji, r=r, L=L: ucb[r][:, ji, 0:L], reads=[wk2, (kucb, 0), (kucb, 1)])
                    mm_group(px, L, 2, lambda ji, jo=jo, wx=wx: wx[:, ji, jo * 128:(jo + 1) * 128],
                             lambda ji, r=r, L=L: ucb[r][:, ji, 0:L], reads=[wk2, (kucb, 0), (kucb, 1)])
                    t1, t4 = TB[si][jo]
                    t2 = ubuf[r][:, jo, :]
                    k1, k4 = [("rgTB", m0, si, jo, q) for q in range(2)]
                    k2a, k2b = (kub, jo, "a"), (kub, jo, "b")
                    P.op("act", lambda e, t1=t1, pa=pa, L=L, c=c: e.activation(out=t1[:, 0:L], in_=ps[pa][:, 0:L], func=AF.Tanh, scale=0.5,
                                                                             bias=hb[:, j * KC + c:j * KC + c + 1]),
                         reads=[("ps", pa), "hb"], writes=[k1])
                    P.op("act", lambda e, t4=t4, px=px, L=L, c=c: e.activation(out=t4[:, 0:L], in_=ps[px][:, 0:L], func=AF.Tanh, scale=0.5,
                                                                             bias=hb[:, n1 + j * KC + c:n1 + j * KC + c + 1]),
                         reads=[("ps", px), "hb"], writes=[k4])
                    P.op("act", lambda e, t1=t1, t2=t2, L=L, c=c: e.activation(out=t2[:, 0:L], in_=t1[:, 0:L], func=AF.Exp,
                                                                             scale=clh[:, j * KC + c:j * KC + c + 1],
                                                                             bias=clh[:, j * KC + c:j * KC + c + 1]),
                         reads=[k1, "clh"], writes=[k2a, k2b])
                    P.op("act", lambda e, t1=t1, L=L, c=c: e.activation(out=t1[:, 0:L], in_=t1[:, 0:L], func=AF.Exp,
                                                                      scale=cl[:, j * KC + c:j * KC + c + 1],
                                                                      bias=cl[:, j * KC + c:j * KC + c + 1]),
                         reads=[k1, "cl"], writes=[k1])
                    P.op("act", lambda e, t1=t1, L=L: e.activation(out=t1[:, 0:L], in_=t1[:, 0:L], func=AF.Sqrt, scale=-0.25, bias=0.25),
                         reads=[k1], writes=[k1])
                    P.op("dve", lambda e, t4=t4, r=r, jo=jo, L=L: e.scalar_tensor_tensor(out=t4[:, 0:L], in0=t4[:, 0:L], scalar=1.0,
                                                                                         in1=uc[r][:, jo, 0:L], op0=ALU.add, op1=ALU.mult),
                         reads=[k4, (kuc, jo)], writes=[k4])
                    P.op("dve", lambda e, t4=t4, t1=t1, L=L: e.tensor_tensor(out=t4[:, 0:L], in0=t4[:, 0:L], in1=t1[:, 0:L], op=ALU.mult),
                         reads=[k4, k1], writes=[k4])
                    P.op("dve", lambda e, t2=t2, t4=t4, t1=t1, L=L, sidx=sidx: e.tensor_tensor_scan(
                        out=t1[:, 0:L], data0=t2[:, 0:L], data1=t4[:, 0:L], initial=hst[:, sidx:sidx + 1], op0=ALU.mult, op1=ALU.add),
                        reads=[k2a, k2b, k4, ("hst", sidx)], writes=[k1])
                    P.op("dve", lambda e, t1=t1, Lr=Lr, sidx=sidx: e.tensor_copy(out=hst[:, sidx:sidx + 1], in_=t1[:, Lr - 1:Lr]),
                         reads=[k1], writes=[("hst", sidx)])
                    P.op("dve", lambda e, t1=t1, r=r, jo=jo, L=L, off=off, c=c: e.tensor_tensor(
                        out=mix[:, c, off:off + L], in0=t1[:, 0:L], in1=ggb[r][:, jo, 0:L], op=ALU.mult),
                        reads=[k1, (kgg, jo)], writes=[("mix", si, c)])
                    lists.append(P.stop())
            return lists

        for b in range(cfg.RGB + 1):
            if b < cfg.RGB:
                wA = loadA(b)
            if b > 0:
                wB = loadB(b - 1)
            rest = []
            for si in range(NS):
                lists = []
                if b < cfg.RGB:
                    lists += stageA(b, si, *wA)
                if b > 0:
                    lists += stageB(b - 1, si, *wB)
                P.interleave([l[:4] for l in lists])
                rest += [l[4:] for l in lists]
            P.interleave(rest)
        out_proj(segs, w_rg_out[j], [[("mix", si, c) for c in range(KC)] for si in range(len(segs))])
        P.barrier()
        A.release(m0)

    def gelu_from_psum(pi, L, out_ap, out_key, ta, ka):
        P.op("act", lambda e: e.activation(out=ta[:, 0:L], in_=ps[pi][:, 0:L], func=AF.Square), reads=[("ps", pi)], writes=[ka])
        P.op("dve", lambda e: e.tensor_scalar(out=ta[:, 0:L], in0=ta[:, 0:L], scalar1=0.044715 * 1.5957691216057308,
                                              scalar2=1.5957691216057308, op0=ALU.mult, op1=ALU.add), reads=[ka], writes=[ka])
        P.op("dve", lambda e: e.tensor_tensor(out=ta[:, 0:L], in0=ta[:, 0:L], in1=ps[pi][:, 0:L], op=ALU.mult),
             reads=[ka, ("ps", pi)], writes=[ka])
        P.op("act", lambda e: e.activation(out=ta[:, 0:L], in_=ta[:, 0:L], func=AF.Sigmoid), reads=[ka], writes=[ka])
        P.op("dve", lambda e: e.tensor_tensor(out=out_ap, in0=ta[:, 0:L], in1=ps[pi][:, 0:L], op=ALU.mult),
             reads=[ka, ("ps", pi)], writes=[out_key])

    def gla(segs, j, pidx):
        m0 = A.mark()
        GT = [[[A.alloc("glT", [128, sg["L"]], F32) for _ in range(3)] for dc in range(2)] for sg in segs]
        nblk = [sg["L"] // 128 for sg in segs]
        boff = [sum(nblk[:i]) for i in range(len(segs))]
        NB = sum(nblk)
        NCH = 2 * NB
        glr = A.alloc("glr", [16, TT], BF16)
        QE = [A.alloc("qeT", [128, 2, TT], BF16) for _ in range(2)]
        KE = [A.alloc("keT", [128, 2, TT], BF16) for _ in range(2)]
        KD = [A.alloc("kdT", [128, 2, TT], BF16) for _ in range(2)]
        kdm = [A.alloc("kdm", [128, NB, 256], BF16) for _ in range(2)]
        vsb = A.alloc("vsb", [128, NB, 512], BF16)
        sgT = A.alloc("sgT", [128, 4, TT], BF16)
        DEC = [A.alloc("dec", [128, 2, NCH], F32) for _ in range(2)]
        Sf = [A.alloc("Sf", [128, 2, 512], F32) for _ in range(2)]
        Sb = [A.alloc("Sb", [128, 2, 512], BF16) for _ in range(3)]
        att = [A.alloc("att", [128, 128], BF16) for _ in range(2)]
        sq = A.alloc("sq", [128, 4, 128], BF16)
        rs2 = A.alloc("rs2", [128, 128], F32)
        rstd2 = A.alloc("rstd2", [128, 128], F32)
        otmp = A.alloc("otmp", [128, 4, 128], F32)
        K = lambda *a: ("gla", m0) + a

        wt, wk = wload(lambda t: [(wview(t, KC, 16), rows(w_gla_in[j])[:, :, 2 * DQ + 2 * D:2 * DQ + 2 * D + 16])])
        wv = wview(wt, KC, 16)
        for si, sg in enumerate(segs):
            off, L = sg["off"], sg["L"]
            pi = newps()

            def f(e, pi=pi, off=off, L=L, wv=wv):
                ins = None
                for k in range(KC):
                    ins = e.matmul(ps[pi][0:16, 0:L], lhsT=wv[:, k, 0:16], rhs=hn[:, k, off:off + L], start=(k == 0), stop=(k == KC - 1))
                return ins
            P.op("pe", f, reads=[wk, ("hn", si)], writes=[("ps", pi)])
            P.op("act", lambda e, pi=pi, off=off, L=L: e.activation(out=glr[0:16, off:off + L], in_=ps[pi][0:16, 0:L], func=AF.Copy),
                 reads=[("ps", pi)], writes=[K("glr", si)])
        P.dma("pool", [(wgk[:, :], w_gla_gk2[j])], reads=[], writes=["wgk"], semname="Wgk")

        sver = [0]
        pending = [None]

        def stage2(po, bo, si, h):
            P.op("act", lambda e: e.activation(out=sq[:, :, :], in_=ps[po][:, :].rearrange("p (c t) -> p c t", t=128), func=AF.Square),
                 reads=[("ps", po)], writes=[K("sq")])
            pn = newps()
            mm_group(pn, 128, 4, lambda vc: ones[:, :], lambda vc: sq[:, vc, :], reads=[K("sq"), "ones"])
            P.op("act", lambda e: e.activation(out=rs2[:, :], in_=ps[pn][:, 0:128], func=AF.Sqrt, bias=EPS, scale=1.0 / 512),
                 reads=[("ps", pn)], writes=[K("rs2")])
            P.op("dve", lambda e: e.reciprocal(out=rstd2[:, :], in_=rs2[:, :]), reads=[K("rs2")], writes=[K("rstd2")])
            P.op("dve", lambda e: e.tensor_tensor(
                out=otmp[:, :, :], in0=ps[po][:, :].rearrange("p (c t) -> p c t", t=128),
                in1=rstd2[:, :].rearrange("p (o t) -> p o t", o=1).broadcast_to([128, 4, 128]), op=ALU.mult),
                reads=[("ps", po), K("rstd2")], writes=[K("otmp")])
            P.op("dve", lambda e: e.tensor_tensor(out=mix[:, h * 4:(h + 1) * 4, bo:bo + 128], in0=otmp[:, :, :], in1=sgT[:, :, bo:bo + 128],
                                                  op=ALU.mult),
                 reads=[K("otmp")] + [K("sg", si, vc) for vc in range(4)], writes=[("mix", si)])

        def pre_load(h):
            wt, wk = wload(lambda t, h=h: [(wview(t, KC, 512)[:, :, 0:256], rows(w_gla_in[j])[:, :, h * 256:(h + 1) * 256]),
                                            (wview(t, KC, 512)[:, :, 256:512], rows(w_gla_in[j])[:, :, DQ + h * 256:DQ + (h + 1) * 256])])
            return wview(wt, KC, 512), wk

        def pre_seg(h, si, wv, wk):
            hp = h % 2
            qeT, keT, kdT, dec = QE[hp], KE[hp], KD[hp], DEC[hp]
            Kq = lambda name, *a_: K(name, hp, *a_)
            sg = segs[si]
            if True:
                off, L = sg["off"], sg["L"]
                nch = L // 64
                ch0 = 2 * boff[si]
                lists = []
                for dc in range(2):
                    P.record()
                    qc = h * 2 + dc
                    pg = 3 * dc
                    P.op("pe", lambda e, pg=pg, qc=qc, off=off, L=L: e.matmul(ps[pg][:, 0:L], lhsT=wgk[0:16, qc * 128:(qc + 1) * 128],
                                                                                rhs=glr[0:16, off:off + L], start=True, stop=True),
                         reads=["wgk", K("glr", si)], writes=[("ps", pg)])
                    t1, t2, t3 = GT[si][dc]
                    k1, k2, k3 = [("glT", m0, si, dc, q_) for q_ in range(3)]
                    P.op("act", lambda e, t1=t1, pg=pg, L=L, qc=qc: e.activation(out=t1[:, 0:L], in_=ps[pg][:, 0:L], func=AF.Exp, scale=-1.0,
                                                                               bias=nbgk[:, j * QC + qc:j * QC + qc + 1]),
                         reads=[("ps", pg), "nbgk"], writes=[k1])
                    P.op("act", lambda e, t1=t1, L=L: e.activation(out=t1[:, 0:L], in_=t1[:, 0:L], func=AF.Ln, bias=1.0), reads=[k1], writes=[k1])
                    P.op("dve", lambda e, t1=t1, t3=t3, L=L: e.tensor_tensor_scan(out=t3[:, 0:L], data0=rmask[:, 0:L], data1=t1[:, 0:L], initial=0.0,
                                                                                  op0=ALU.mult, op1=ALU.add), reads=[k1, "rmask"], writes=[k3])
                    P.op("act", lambda e, t3=t3, L=L, dc=dc, ch0=ch0, nch=nch: e.activation(
                        out=dec[:, dc, ch0:ch0 + nch], in_=t3[:, 0:L].rearrange("p (n c) -> p n c", c=64)[:, :, 63], func=AF.Exp, scale=-1.0 / 16),
                        reads=[k3], writes=[Kq("dec", si, dc)])
                    pq = 3 * dc + 1
                    mm_group(pq, L, KC, lambda k, dc=dc, wv=wv: wv[:, k, dc * 128:(dc + 1) * 128],
                             lambda k, off=off, L=L: hn[:, k, off:off + L], reads=[wk, ("hn", si)])
                    P.op("act", lambda e, t2=t2, t3=t3, L=L: e.activation(out=t2[:, 0:L], in_=t3[:, 0:L], func=AF.Exp, scale=-1.0 / 16),
                         reads=[k3], writes=[k2])
                    P.op("dve", lambda e, t2=t2, pq=pq, L=L, off=off, dc=dc: e.scalar_tensor_tensor(
                        out=qeT[:, dc, off:off + L], in0=ps[pq][:, 0:L], scalar=1.0 / 16.0, in1=t2[:, 0:L], op0=ALU.mult, op1=ALU.mult),
                        reads=[("ps", pq), k2], writes=[Kq("qeT", si, dc)])
                    pk = 3 * dc + 2
                    mm_group(pk, L, KC, lambda k, dc=dc, wv=wv: wv[:, k, 256 + dc * 128:256 + (dc + 1) * 128],
                             lambda k, off=off, L=L: hn[:, k, off:off + L], reads=[wk, ("hn", si)])
                    P.op("act", lambda e, t2=t2, t3=t3, L=L: e.activation(out=t2[:, 0:L], in_=t3[:, 0:L], func=AF.Exp, scale=1.0 / 16),
                         reads=[k3, Kq("qeT", si, dc)], writes=[k2])
                    P.op("dve", lambda e, t2=t2, pk=pk, L=L, off=off, dc=dc: e.tensor_tensor(
                        out=keT[:, dc, off:off + L], in0=ps[pk][:, 0:L], in1=t2[:, 0:L], op=ALU.mult),
                        reads=[("ps", pk), k2], writes=[Kq("keT", si, dc)])
                    P.op("dve", lambda e, t1=t1, t3=t3, L=L, nch=nch: e.tensor_tensor(
                        out=t1[:, 0:L].rearrange("p (n c) -> p n c", c=64), in0=t3[:, 0:L].rearrange("p (n c) -> p n c", c=64),
                        in1=t3[:, 0:L].rearrange("p (n c) -> p n c", c=64)[:, :, 63:64].broadcast_to([128, nch, 64]), op=ALU.subtract),
                        reads=[k3], writes=[k1])
                    P.op("act", lambda e, t1=t1, L=L: e.activation(out=t1[:, 0:L], in_=t1[:, 0:L], func=AF.Exp, scale=1.0 / 16), reads=[k1], writes=[k1])
                    P.op("dve", lambda e, t1=t1, pk=pk, L=L, off=off, dc=dc: e.tensor_tensor(
                        out=kdT[:, dc, off:off + L], in0=ps[pk][:, 0:L], in1=t1[:, 0:L], op=ALU.mult),
                        reads=[("ps", pk), k1], writes=[Kq("kdT", si, dc)])
                    lists.append(P.stop())
                P.interleave(lists)
        def head_main(h):
            hp = h % 2
            qeT, keT, kdT, dec = QE[hp], KE[hp], KD[hp], DEC[hp]
            Kq = lambda name, *a_: K(name, hp, *a_)
            wt, wk = wload(lambda t, h=h: [(wview(t, KC, 512), rows(w_gla_in[j])[:, :, 2 * DQ + h * 512:2 * DQ + (h + 1) * 512])])
            wv = wview(wt, KC, 512)
            for si, sg in enumerate(segs):
                off = sg["off"]
                for b in range(nblk[si]):
                    bo = off + b * 128
                    bi = boff[si] + b
                    pv = newps()
                    mm_group(pv, 512, KC, lambda k, bo=bo: hn[:, k, bo:bo + 128], lambda k, wv=wv: wv[:, k, :], reads=[wk, ("hn", si)])
                    P.op("act", lambda e, pv=pv, bi=bi: e.activation(out=vsb[:, bi, :], in_=ps[pv][:, :], func=AF.Copy),
                         reads=[("ps", pv)], writes=[K("v", bi)])
            wt, wk = wload(lambda t, h=h: [(wview(t, KC, 512), rows(w_gla_in[j])[:, :, 2 * DQ + D + h * 512:2 * DQ + D + (h + 1) * 512])])
            wv = wview(wt, KC, 512)
            wg_v, wg_k = wv, wk
            fillers = {si: [] for si in range(len(segs))}

            def g_group(vc, si, wv=wg_v, wk=wg_k):
                sg = segs[si]
                off, L = sg["off"], sg["L"]
                pg = newps()
                mm_group(pg, L, KC, lambda k: wv[:, k, vc * 128:(vc + 1) * 128],
                         lambda k: hn[:, k, off:off + L], reads=[wk, ("hn", si)])
                P.op("act", lambda e: e.activation(out=sgT[:, vc, off:off + L], in_=ps[pg][:, 0:L], func=AF.Silu),
                     reads=[("ps", pg)], writes=[K("sg", si, vc)])
                P.op("dve", lambda e: e.tensor_scalar(out=sgT[:, vc, off:off + L], in0=sgT[:, vc, off:off + L], scalar1=vcol("gnw", j * 4 + vc),
                                                      scalar2=None, op0=ALU.mult), reads=[K("sg", si, vc), "vecs"], writes=[K("sg", si, vc)])
            forder = []
            for seq_ in (0, 1):
                for si, sg in enumerate(segs):
                    if sg["seq"] == seq_:
                        for vc in range(4):
                            forder.append(("g", si, vc))
                        if h + 1 < H:
                            forder.append(("pre", si, None))
            fpos = [0]
            nxt = [None]

            def fill(n=None, upto_si=None):
                while fpos[0] < len(forder):
                    if n is not None and n <= 0:
                        break
                    if upto_si is not None and not any(it[0] == "g" and it[1] == upto_si for it in forder[fpos[0]:]):
                        break
                    kind, s, vc = forder[fpos[0]]
                    fpos[0] += 1
                    if kind == "g":
                        g_group(vc, s)
                    else:
                        if nxt[0] is None:
                            nxt[0] = pre_load(h + 1)
                        pre_seg(h + 1, s, *nxt[0])
                    if n is not None:
                        n -= 1
            for si, sg in enumerate(segs):
                off = sg["off"]
                for b in range(nblk[si]):
                    bo = off + b * 128
                    bi = boff[si] + b
                    pt = newps()

                    def f(e, pt=pt, bo=bo):
                        ins = None
                        for dc in range(2):
                            ins = e.transpose(out=psb[pt][:, dc * 128:(dc + 1) * 128], in_=kdT[:, dc, bo:bo + 128], identity=ident[:, :])
                        return ins
                    P.op("pe", f, reads=[Kq("kdT", si, 0), Kq("kdT", si, 1), "ident"], writes=[("ps", pt)])
                    P.op("dve", lambda e, pt=pt, bi=bi: e.tensor_scalar(out=kdm[0][:, bi, :], in0=psb[pt][:, 0:256], scalar1=rmA[:, 0:1],
                                                                         scalar2=None, op0=ALU.mult), reads=[("ps", pt), "rmA"], writes=[K("kdmA", bi)])
                    P.op("dve", lambda e, pt=pt, bi=bi: e.tensor_scalar(out=kdm[1][:, bi, :], in0=psb[pt][:, 0:256], scalar1=rmB[:, 0:1],
                                                                         scalar2=None, op0=ALU.mult), reads=[("ps", pt), "rmB"], writes=[K("kdmB", bi)])
            for seq in (0, 1):
                ssegs = [(si, sg) for si, sg in enumerate(segs) if sg["seq"] == seq]
                if not ssegs:
                    continue
                S = Sf[seq]
                kS = K("Sf", seq)
                dkey = ("dS", j, h, seq)
                Sview = S[:, :, :]
                kS2 = [(kS, 0), (kS, 1)]
                if seq == 1:
                    P.dma("sp", [(Sview, sgla_d[j, h].rearrange("(c p) v -> p c v", p=128))], reads=[], writes=kS2, semname=f"LS{seq}")
                elif pidx == 0:
                    P.op("dve", lambda e, S=S: e.memset(S[:, :, :], 0.0), writes=kS2)
                else:
                    P.dma("sp", [(Sview, o_gla[0][j, h].rearrange("(c p) v -> p c v", p=128))], reads=[dkey], writes=kS2, semname=f"LS{seq}")
                v0 = sver[0]
                P.op("act", lambda e, S=S, v0=v0: e.activation(out=Sb[v0 % 3][:, :, :], in_=S[:, :, :], func=AF.Copy),
                     reads=kS2, writes=[K("Sb", v0 % 3, 0), K("Sb", v0 % 3, 1)])
                for si, sg in ssegs:
                    off, L, Lr = sg["off"], sg["L"], sg["Lr"]
                    for b in range(nblk[si]):
                        bo = off + b * 128
                        bi = boff[si] + b
                        vers = []
                        for half in range(2):
                            vers.append(sver[0])
                            if b * 128 + half * 64 >= Lr:
                                continue
                            ch = 2 * bi + half
                            vn = sver[0] + 1
                            for dc in range(2):
                                pd = newps()
                                P.op("pe", lambda e, pd=pd, half=half, bi=bi, dc=dc: e.matmul(
                                    ps[pd][:, :], lhsT=kdm[half][:, bi, dc * 128:(dc + 1) * 128], rhs=vsb[:, bi, :], start=True, stop=True),
                                    reads=[K("kdmA" if half == 0 else "kdmB", bi), K("v", bi)], writes=[("ps", pd)])
                                P.op("dve", lambda e, S=S, dc=dc, ch=ch, pd=pd: e.scalar_tensor_tensor(
                                    out=S[:, dc, :], in0=S[:, dc, :], scalar=dec[:, dc, ch:ch + 1], in1=ps[pd][:, :], op0=ALU.mult, op1=ALU.add),
                                    reads=[(kS, dc), Kq("dec", si, dc), ("ps", pd)], writes=[(kS, dc)])
                                P.op("act", lambda e, S=S, dc=dc, vn=vn: e.activation(out=Sb[vn % 3][:, dc, :], in_=S[:, dc, :], func=AF.Copy),
                                     reads=[(kS, dc)], writes=[K("Sb", vn % 3, dc)])
                            sver[0] = vn
                        pa = newps()
                        mm_group(pa, 128, 2, lambda dc, bo=bo: keT[:, dc, bo:bo + 128], lambda dc, bo=bo: qeT[:, dc, bo:bo + 128],
                                 reads=[Kq("keT", si, 0), Kq("keT", si, 1), Kq("qeT", si, 0), Kq("qeT", si, 1)])
                        ab = bi % 2
                        P.op("dve", lambda e, pa=pa, ab=ab: e.tensor_tensor(out=att[ab][:, :], in0=ps[pa][:, 0:128], in1=mask[:, :], op=ALU.mult),
                             reads=[("ps", pa), "mask"], writes=[K("att", ab)])
                        po = newpo()
                        fill(n=2)

                        def fo(e, po=po, bi=bi, bo=bo, ab=ab, vers=tuple(vers)):
                            ins = None
                            for vc in range(4):
                                ov = ps[po][:, vc * 128:(vc + 1) * 128]
                                e.matmul(ov, lhsT=vsb[:, bi, vc * 128:(vc + 1) * 128], rhs=att[ab][:, :], start=True, stop=False)
                                for half in range(2):
                                    sbv = Sb[vers[half] % 3]
                                    for dc in range(2):
                                        last = (half == 1 and dc == 1)
                                        ins = e.matmul(ov[:, half * 64:(half + 1) * 64], lhsT=sbv[:, dc, vc * 128:(vc + 1) * 128],
                                                       rhs=qeT[:, dc, bo + half * 64:bo + (half + 1) * 64], start=False, stop=last)
                            return ins
                        P.op("pe", fo, reads=[K("v", bi), K("att", ab), K("Sb", vers[0] % 3, 0), K("Sb", vers[0] % 3, 1), K("Sb", vers[1] % 3, 0), K("Sb", vers[1] % 3, 1),
                                    Kq("qeT", si, 0), Kq("qeT", si, 1)],
                             writes=[("ps", po)])
                        if pending[0] is not None:
                            fill(upto_si=pending[0][2])
                            stage2(*pending[0])
                        pending[0] = (po, bo, si, h)
                if pending[0] is not None:
                    fill(upto_si=pending[0][2])
                    stage2(*pending[0])
                    pending[0] = None
                P.dma("sp", [(o_gla[seq][j, h].rearrange("(c p) v -> p c v", p=128), Sview)], reads=[(kS, 0), (kS, 1)], writes=[dkey], semname=f"SS{seq}")
            fill()
        w0 = pre_load(0)
        for si in range(len(segs)):
            pre_seg(0, si, *w0)
        for h in range(H):
            head_main(h)
        out_proj(segs, w_gla_out[j])
        P.barrier()
        A.release(m0)

    for pidx, pss in enumerate(cfg.passes):
        segs = []
        off = 0
        for (seq, t0, L, Lr) in pss:
            segs.append(dict(seq=seq, t0=t0, L=L, Lr=Lr, off=off))
            off += L
        for si, sg in enumerate(segs):
            o_, L, Lr = sg["off"], sg["L"], sg["Lr"]
            xk = [("x", si, m) for m in range(KC)]
            P.dma("sp", [(xT[:, :, o_:o_ + Lr], xin[sg["seq"]][:, sg["t0"]:sg["t0"] + Lr].rearrange("(k p) t -> p k t", p=128))],
                  reads=[], writes=xk, semname=f"Lx{si}")
            if Lr < L:
                P.op("dve", lambda e, o_=o_, L=L, Lr=Lr: e.memset(xT[:, :, o_ + Lr:o_ + L], 0.0), reads=xk, writes=xk)
        for l in range(cfg.DEPTH):
            j = l // 2
            rmsnorm(segs, "nm", l * KC, hn, lambda si, k: ("hn", si), sqb)
            if l % 2 == 0:
                rglru(segs, j)
            else:
                gla(segs, j, pidx)
            rmsnorm(segs, "nf", l * KC, hn, lambda si, k: ("hn", si), sqb)
            ffn(segs, l)
        rmsnorm(segs, "nfin", 0, xT, lambda si, k: ("x", si, k), sqb)
        for si, sg in enumerate(segs):
            o_, Lr = sg["off"], sg["Lr"]
            xk = [("x", si, m) for m in range(KC)]
            P.dma("sp", [(yout[sg["seq"]][:, sg["t0"]:sg["t0"] + Lr].rearrange("(k p) t -> p k t", p=128), xT[:, :, o_:o_ + Lr])],
                  reads=xk, writes=[("y", si)], semname=f"SY{si}")
    P.dma("sp", [(o_h[:, :], hst[:, :])], reads=["hst"] + [("hst", i) for i in range(2 * NRG * KC)], writes=["o_h"], semname="SH")
    P.dma("sp", [(o_conv[:, :], cst[:, :])], reads=["cst"] + [("cst", i) for i in range(2 * NRG * KC)], writes=["o_conv"], semname="SC")
    P.final_wait("sp")
    P.emit()
    cfg.arena_peak = A.peak
    return nc


def _pc(v):
    v = np.asarray(v, np.float32)
    lead = v.shape[:-1]
    C = v.shape[-1] // 128
    a = v.reshape(lead + (C, 128))
    a = np.moveaxis(a, -1, 0)
    return np.ascontiguousarray(a.reshape(128, -1))


def pack_vecs(cfg, inp, b):
    parts = [
        _pc(inp["norm_mix"]), _pc(inp["norm_ffn"]), _pc(inp["norm_final"]),
        _pc(inp["rg_conv_w"]), _pc(inp["rg_conv_b"]), _pc(inp["rg_b_a"]), _pc(inp["rg_b_x"]), _pc(inp["rg_lambda"]),
        _pc(inp["gla_b_gk"]), _pc(inp["gla_norm_w"]),
        _pc(inp["state_rglru_h"][:, b, :]),
        np.ascontiguousarray(np.transpose(np.asarray(inp["state_rglru_conv"][:, b], np.float32).reshape(cfg.NRG, 3, cfg.KC, 128),
                                          (3, 0, 2, 1)).reshape(128, -1)),
    ]
    out = np.concatenate(parts, axis=1)
    assert out.shape[1] == cfg.NV, (out.shape, cfg.NV)
    return out


WNAMES = ["rg_w_in", "rg_w_a", "rg_w_x", "rg_w_out", "gla_w_in", "gla_w_gk2", "gla_w_out", "ffn_w_up", "ffn_w_down"]


def make_in_maps(cfg, inp, ncores):
    maps = []
    w = {k: np.ascontiguousarray(np.asarray(inp[k], np.float32)) for k in WNAMES}
    for b in range(ncores):
        m = dict(w)
        m["xp"] = np.ascontiguousarray(np.asarray(inp["x_prompt"][b], np.float32).T)
        m["xs"] = np.ascontiguousarray(np.asarray(inp["x_sample"][b], np.float32).T)
        m["vecs"] = pack_vecs(cfg, inp, b)
        m["sgla"] = np.ascontiguousarray(np.asarray(inp["state_gla"][:, b], np.float32))
        maps.append(m)
    return maps


def gather(cfg, results):
    nb = len(results)
    KC, NRG = cfg.KC, cfg.NRG
    y_p = np.stack([r["yp"].T for r in results]).astype(np.float32)
    y_s = np.stack([r["ys"].T for r in results]).astype(np.float32)
    h = np.stack([r["o_h"].reshape(128, 2, NRG, KC) for r in results])
    h = np.transpose(h, (2, 3, 0, 4, 1)).reshape(2, NRG, nb, KC * 128)
    c = np.stack([r["o_conv"].reshape(128, 2, NRG, KC, 3) for r in results])
    c = np.transpose(c, (2, 3, 0, 5, 4, 1)).reshape(2, NRG, nb, 3, KC * 128)
    gp = np.stack([r["o_gla_p"] for r in results], axis=1)
    gs = np.stack([r["o_gla_s"] for r in results], axis=1)
    f = lambda a: np.ascontiguousarray(a, dtype=np.float32)
    return (f(y_p), f(y_s), f(h[0]), f(c[0]), f(gp), f(h[1]), f(c[1]), f(gs))


def kernel(**inputs):
    cfg = Cfg()
    nc = build(cfg)
    maps = make_in_maps(cfg, inputs, 8)
    res = run_bass_kernel_spmd(nc, maps, core_ids=list(range(8)))
    return gather(cfg, res.results)
```

```python
import numpy as np
import concourse.bass as bass
import concourse.mybir as mybir
from concourse.bass_utils import run_bass_kernel_spmd

F32 = mybir.dt.float32
BF16 = mybir.dt.bfloat16
AF = mybir.ActivationFunctionType
ALU = mybir.AluOpType
EPS = 1e-6


class Cfg:
    def __init__(s, D=2048, SEQ=2048, DSEQ=64, DEPTH=4, passes=None, nw=3):
        s.D, s.SEQ, s.DSEQ, s.DEPTH = D, SEQ, DSEQ, DEPTH
        s.KC = D // 128
        s.NRG = (DEPTH + 1) // 2
        s.NGLA = DEPTH // 2
        s.RGB = D // 256
        s.H = D // 512
        s.DQ = s.H * 256
        s.QC = s.DQ // 128
        s.GIN = 2 * s.DQ + 2 * D + 16
        s.DFF = ((8 * D // 3 + 255) // 256) * 256
        s.FC = s.DFF // 128
        s.NW = nw
        if passes is None:
            passes = []
            t = 0
            first = True
            while t < SEQ:
                L = min(512, SEQ - t)
                segs = [(0, t, L, L)]
                if first:
                    segs.append((1, 0, 128, DSEQ))
                    first = False
                passes.append(segs)
                t += L
        s.passes = passes
        s.TT = max(sum(g[2] for g in p) for p in passes)
        o = {}
        c = 0
        for name, n in [("nm", DEPTH * s.KC), ("nf", DEPTH * s.KC), ("nfin", s.KC),
                        ("cw", s.NRG * 4 * s.KC), ("cb", s.NRG * s.KC), ("ba", s.NRG * s.KC),
                        ("bx", s.NRG * s.KC), ("lam", s.NRG * s.KC), ("bgk", max(1, s.NGLA) * s.QC),
                        ("gnw", max(1, s.NGLA) * 4), ("h0", s.NRG * s.KC), ("c0", s.NRG * s.KC * 3)]:
            o[name] = c
            c += n
        s.vo = o
        s.NV = c


class Prog:
    ENG = ["pe", "act", "dve", "pool", "sp"]

    def __init__(self, nc):
        self.nc = nc
        self.lists = {e: [] for e in self.ENG}
        self.sems = {}
        self.cnt = {e: 0 for e in self.ENG}
        self.known = {e: {} for e in self.ENG}
        self.res = {}
        self.dmacnt = {}
        self.nps = 0
        self.rec = None

    def sem(self, name):
        if name not in self.sems:
            self.sems[name] = self.nc.alloc_semaphore(name)
        return self.sems[name]

    def _deps(self, reads, writes):
        evs = []
        for k in reads:
            r = self.res.get(k)
            if r and r[0]:
                evs.append(r[0])
        for k in writes:
            r = self.res.get(k)
            if r:
                if r[0]:
                    evs.append(r[0])
                for sn, (v, se) in r[1].items():
                    evs.append((sn, v, se))
        return evs

    def _waits(self, eng, evs):
        need = {}
        for (sn, v, se) in evs:
            if se == eng and eng == "pe":
                continue
            if v > self.known[eng].get(sn, 0) and v > need.get(sn, 0):
                need[sn] = v
        for sn, v in need.items():
            self.known[eng][sn] = v
            sem = self.sem(sn)
            self.lists[eng].append(lambda e, sem=sem, v=v: e.wait_ge(sem, v))

    def _record(self, ev, reads, writes):
        for k in reads:
            r = self.res.setdefault(k, [None, {}])
            old = r[1].get(ev[0])
            if old is None or old[0] < ev[1]:
                r[1][ev[0]] = (ev[1], ev[2])
        for k in writes:
            self.res[k] = [ev, {}]

    def record(self):
        self.rec = []
        return self.rec

    def stop(self):
        r, self.rec = self.rec, None
        return r

    def interleave(self, lists):
        k = 0
        more = True
        while more:
            more = False
            for l in lists:
                if k < len(l):
                    more = True
                    it = l[k]
                    if it[0] == "op":
                        self.op(*it[1:])
                    else:
                        self.dma(*it[1:])
            k += 1

    def op(self, eng, fn, reads=(), writes=()):
        if self.rec is not None:
            self.rec.append(("op", eng, fn, tuple(reads), tuple(writes)))
            return
        self._waits(eng, self._deps(reads, writes))
        self.cnt[eng] += 1
        sn = "S_" + eng
        sem = self.sem(sn)
        self.lists[eng].append(lambda e, fn=fn, sem=sem: fn(e).then_inc(sem, 1))
        self._record((sn, self.cnt[eng], eng), reads, writes)

    def dma(self, q, pairs, reads, writes, semname):
        if self.rec is not None:
            self.rec.append(("dma", q, pairs, tuple(reads), tuple(writes), semname))
            return
        self._waits(q, self._deps(reads, writes))
        sem = self.sem(semname)
        for (o, i) in pairs:
            self.dmacnt[semname] = self.dmacnt.get(semname, 0) + 1
            self.lists[q].append(lambda e, o=o, i=i, sem=sem: e.dma_start(out=o, in_=i).then_inc(sem, 16))
        self._record((semname, 16 * self.dmacnt[semname], "dma"), reads, writes)

    def barrier(self, engs=("pe", "act", "dve", "sp")):
        evs = []
        for e in ("pe", "act", "dve", "pool"):
            if self.cnt[e]:
                evs.append(("S_" + e, self.cnt[e], e))
        for sn, n in self.dmacnt.items():
            if not sn.startswith("W"):
                evs.append((sn, 16 * n, "dma"))
        for e in engs:
            self._waits(e, [ev for ev in evs if not (ev[2] == e and e == "pe")])

    def final_wait(self, eng="sp"):
        evs = [(sn, 16 * n, "dma") for sn, n in self.dmacnt.items()]
        for e in ("pe", "act", "dve", "pool"):
            if self.cnt[e]:
                evs.append(("S_" + e, self.cnt[e], e))
        self._waits(eng, evs)

    def emit(self):
        nc = self.nc
        with nc.Block() as block:
            for name, attr in [("sp", "sync"), ("pool", "gpsimd"), ("pe", "tensor"),
                               ("act", "scalar"), ("dve", "vector")]:
                lst = self.lists[name]
                if not lst:
                    continue

                def body(e, lst=lst):
                    for f in lst:
                        f(e)
                getattr(block, attr)(body)


class Arena:
    def __init__(self, nc, base, size):
        self.nc, self.base, self.size, self.top = nc, base, size, 0
        self.n = 0
        self.peak = 0

    def alloc(self, name, shape, dtype):
        esz = 2 if dtype == BF16 else 4
        nb = esz
        for d in shape[1:]:
            nb *= d
        nb = (nb + 31) // 32 * 32
        assert self.top + nb <= self.size, f"arena overflow {name}: {self.top}+{nb}>{self.size}"
        self.n += 1
        t = self.nc.alloc_sbuf_tensor_at(f"{name}_{self.n}", list(shape), dtype, offset=self.base + self.top)
        self.top += nb
        self.peak = max(self.peak, self.top)
        return t

    def mark(self):
        return self.top

    def release(self, m):
        self.top = m


def build(cfg):
    nc = bass.Bass("TRN2", target_bir_lowering=False)
    P = Prog(nc)
    D, KC, TT = cfg.D, cfg.KC, cfg.TT
    NRG, NGLA, H, DQ, QC, FC, DFF = cfg.NRG, cfg.NGLA, cfg.H, cfg.DQ, cfg.QC, cfg.FC, cfg.DFF
    vo = cfg.vo

    def din(name, shape):
        return nc.dram_tensor(name, list(shape), F32, kind="ExternalInput").ap()

    def dout(name, shape):
        return nc.dram_tensor(name, list(shape), F32, kind="ExternalOutput").ap()

    xin = [din("xp", [D, cfg.SEQ]), din("xs", [D, cfg.DSEQ])]
    vecs_d = din("vecs", [128, cfg.NV])
    sgla_d = din("sgla", [max(1, NGLA), H, 256, 512])
    w_rg_in = din("rg_w_in", [NRG, D, 2 * D])
    w_rg_a = din("rg_w_a", [NRG, cfg.RGB, 256, 256])
    w_rg_x = din("rg_w_x", [NRG, cfg.RGB, 256, 256])
    w_rg_out = din("rg_w_out", [NRG, D, D])
    w_gla_in = din("gla_w_in", [max(1, NGLA), D, cfg.GIN])
    w_gla_gk2 = din("gla_w_gk2", [max(1, NGLA), 16, DQ])
    w_gla_out = din("gla_w_out", [max(1, NGLA), D, D])
    w_up = din("ffn_w_up", [cfg.DEPTH, D, 2 * DFF])
    w_dn = din("ffn_w_down", [cfg.DEPTH, DFF, D])
    yout = [dout("yp", [D, cfg.SEQ]), dout("ys", [D, cfg.DSEQ])]
    o_h = dout("o_h", [128, 2 * NRG * KC])
    o_conv = dout("o_conv", [128, 2 * NRG * KC * 3])
    o_gla = [dout("o_gla_p", [max(1, NGLA), H, 256, 512]), dout("o_gla_s", [max(1, NGLA), H, 256, 512])]


    arena_bytes = 207 * 1024
    nc.alloc_sbuf_tensor("arena", [128, arena_bytes // 4], F32)
    A = Arena(nc, nc.lookup_mloc("arena").addr, arena_bytes)
    NGm = max(1, NGLA)
    vecs = A.alloc("vecs", [128, cfg.NV], F32)
    cl = A.alloc("cl", [128, NRG * KC], F32)
    cl2 = A.alloc("cl2", [128, NRG * KC], F32)
    nbgk = A.alloc("nbgk", [128, NGm * QC], F32)
    clh = A.alloc("clh", [128, NRG * KC], F32)
    hb = A.alloc("hb", [128, 2 * NRG * KC], F32)
    hst = A.alloc("hst", [128, 2 * NRG * KC], F32)
    cst = A.alloc("cst", [128, 2 * NRG * KC * 3], F32)
    ident = A.alloc("ident", [128, 128], BF16)
    zer = A.alloc("zer", [128, 128], BF16)
    ones = A.alloc("ones", [128, 128], BF16)
    onesf = A.alloc("onesf", [128, 128], F32)
    mask = A.alloc("mask", [128, 128], F32)
    rmA = A.alloc("rmA", [128, 1], F32)
    rmB = A.alloc("rmB", [128, 1], F32)
    rmask = A.alloc("rmask", [128, 512], F32)
    rs = A.alloc("rs", [128, 512], F32)
    rstd = A.alloc("rstd", [128, 512], F32)
    xT = A.alloc("xT", [128, KC, TT], F32)
    hn = A.alloc("hn", [128, KC, TT], BF16)
    mix_off = A.base + A.top
    mix = A.alloc("mix", [128, KC, TT], BF16)
    sqb = nc.alloc_sbuf_tensor_at("sqb_alias", [128, KC * TT], BF16, offset=mix_off)
    WSL = 8192
    wgk = A.alloc("wgk", [16, DQ], BF16)
    wr = [A.alloc(f"wr{i}", [128, WSL], BF16) for i in range(cfg.NW)]
    ps = [nc.alloc_psum_tensor(f"ps{i}", [128, 512], F32) for i in range(8)]
    psb = [p.bitcast(BF16) for p in ps]
    st = {"wn": 0, "ps": 0, "po": 0}

    def newps():
        i = st["ps"] % 6
        st["ps"] += 1
        return i

    def newpo():
        i = 6 + st["po"] % 2
        st["po"] += 1
        return i

    def wload(pairs_fn):
        i = st["wn"] % cfg.NW
        st["wn"] += 1
        P.dma("pool", pairs_fn(wr[i]), reads=[], writes=[("w", i)], semname=f"W{i}")
        return wr[i], ("w", i)

    def wview(t, kcn, w):
        return t[:, 0:kcn * w].rearrange("p (k n) -> p k n", n=w)

    def rows(w2d):
        return w2d.rearrange("(k p) n -> p k n", p=128)

    def vcol(name, i):
        return vecs[:, vo[name] + i: vo[name] + i + 1]

    P.dma("sp", [(vecs[:, :], vecs_d[:, :])], reads=[], writes=["vecs"], semname="Lv")
    n1 = NRG * KC
    P.op("dve", lambda e: e.memset(hst[:, 0:n1], 0.0), writes=["hst"])
    P.op("dve", lambda e: e.tensor_copy(out=hst[:, n1:2 * n1], in_=vecs[:, vo["h0"]:vo["h0"] + n1]),
         reads=["vecs"], writes=["hst"])
    P.op("dve", lambda e: e.memset(cst[:, 0:3 * n1], 0.0), writes=["cst"])
    P.op("dve", lambda e: e.tensor_copy(out=cst[:, 3 * n1:6 * n1], in_=vecs[:, vo["c0"]:vo["c0"] + 3 * n1]),
         reads=["vecs"], writes=["cst"])
    P.op("act", lambda e: e.activation(out=cl[:, :], in_=vecs[:, vo["lam"]:vo["lam"] + n1], func=AF.Exp, scale=-1.0),
         reads=["vecs"], writes=["cl"])
    P.op("act", lambda e: e.activation(out=cl[:, :], in_=cl[:, :], func=AF.Ln, bias=1.0), reads=["cl"], writes=["cl"])
    P.op("dve", lambda e: e.tensor_scalar(out=cl2[:, :], in0=cl[:, :], scalar1=-16.0, scalar2=None, op0=ALU.mult),
         reads=["cl"], writes=["cl2"])
    P.op("dve", lambda e: e.tensor_scalar(out=cl[:, :], in0=cl[:, :], scalar1=-8.0, scalar2=None, op0=ALU.mult),
         reads=["cl", "cl2"], writes=["cl"])
    P.op("dve", lambda e: e.tensor_scalar(out=clh[:, :], in0=cl[:, :], scalar1=0.5, scalar2=None, op0=ALU.mult),
         reads=["cl"], writes=["clh"])
    P.op("dve", lambda e: e.tensor_scalar(out=hb[:, 0:n1], in0=vecs[:, vo["ba"]:vo["ba"] + n1], scalar1=0.5, scalar2=None, op0=ALU.mult),
         reads=["vecs"], writes=["hb"])
    P.op("dve", lambda e: e.tensor_scalar(out=hb[:, n1:2 * n1], in0=vecs[:, vo["bx"]:vo["bx"] + n1], scalar1=0.5, scalar2=None, op0=ALU.mult),
         reads=["vecs", "hb"], writes=["hb"])
    P.op("dve", lambda e: e.tensor_scalar(out=nbgk[:, :], in0=vecs[:, vo["bgk"]:vo["bgk"] + NGm * QC], scalar1=-1.0,
                                          scalar2=None, op0=ALU.mult), reads=["vecs"], writes=["nbgk"])
    P.op("dve", lambda e: e.memset(zer[:, :], 0.0), writes=["zer"])
    P.op("dve", lambda e: e.memset(ones[:, :], 1.0), writes=["ones"])
    P.op("dve", lambda e: e.memset(onesf[:, :], 1.0), writes=["onesf"])
    P.op("pool", lambda e: e.affine_select(out=ident[:, :], in_=zer[:, :], pattern=[[1, 128]], compare_op=ALU.not_equal,
                                           fill=1.0, base=0, channel_multiplier=-1), reads=["zer"], writes=["ident"])
    P.op("pool", lambda e: e.affine_select(out=mask[:, :], in_=onesf[:, :], pattern=[[1, 128]], compare_op=ALU.is_ge,
                                           fill=0.0, base=0, channel_multiplier=-1), reads=["onesf"], writes=["mask"])
    P.op("dve", lambda e: e.memset(mask[0:64, 64:128], 0.0), reads=["mask"], writes=["mask"])
    P.op("dve", lambda e: e.memset(rmA[0:64, :], 1.0), writes=["rmA"])
    P.op("dve", lambda e: e.memset(rmA[64:128, :], 0.0), reads=["rmA"], writes=["rmA"])
    P.op("dve", lambda e: e.memset(rmB[0:64, :], 0.0), writes=["rmB"])
    P.op("dve", lambda e: e.memset(rmB[64:128, :], 1.0), reads=["rmB"], writes=["rmB"])
    P.op("dve", lambda e: e.memset(rmask[:, :], 1.0), writes=["rmask"])
    P.op("dve", lambda e: e.memset(rmask[:, :].rearrange("p (n c) -> p n c", c=64)[:, :, 0:1], 0.0),
         reads=["rmask"], writes=["rmask"])

    P.barrier(("pe", "act", "dve", "sp", "pool"))

    def mm_group(pi, L, kcn, lhs_fn, rhs_fn, reads, col0=0):
        def f(e):
            ins = None
            for k in range(kcn):
                ins = e.matmul(ps[pi][:, col0:col0 + L], lhsT=lhs_fn(k), rhs=rhs_fn(k), start=(k == 0), stop=(k == kcn - 1))
            return ins
        P.op("pe", f, reads=reads, writes=[("ps", pi)])

    def rmsnorm(segs, wname, wbase, out_t, out_key, sqb):
        KH = KC // 2
        for si, sg in enumerate(segs):
            off, L = sg["off"], sg["L"]
            xk = [("x", si, m) for m in range(KC)]
            sqv = sqb[:, 0:KC * L].rearrange("p (k t) -> p k t", t=L)
            mixk = [("mix", s2) for s2 in range(len(segs))] + [("mix", s2, c) for s2 in range(len(segs)) for c in range(KC)]
            P.op("act", lambda e, sqv=sqv, off=off, L=L: e.activation(out=sqv[:, 0:KH, :], in_=xT[:, 0:KH, off:off + L], func=AF.Square),
                 reads=xk[0:KH], writes=["sqbA"] + mixk)
            P.op("dve", lambda e, sqv=sqv, off=off, L=L: e.tensor_tensor(out=sqv[:, KH:KC, :], in0=xT[:, KH:KC, off:off + L],
                                                                         in1=xT[:, KH:KC, off:off + L], op=ALU.mult),
                 reads=xk[KH:KC], writes=["sqbB"] + mixk)
            pi = newps()
            mm_group(pi, L, KC, lambda k: ones[:, :], lambda k, sqv=sqv: sqv[:, k, :], reads=["sqbA", "sqbB", "ones"] + mixk)
            P.op("act", lambda e, pi=pi, L=L: e.activation(out=rs[:, 0:L], in_=ps[pi][:, 0:L], func=AF.Ln, bias=EPS, scale=1.0 / D),
                 reads=[("ps", pi)], writes=["rs"])
            P.op("act", lambda e, L=L: e.activation(out=rstd[:, 0:L], in_=rs[:, 0:L], func=AF.Exp, scale=-0.5), reads=["rs"], writes=["rstd"])
            for k in range(KC):
                P.op("dve", lambda e, k=k, off=off, L=L: e.scalar_tensor_tensor(
                    out=out_t[:, k, off:off + L], in0=xT[:, k, off:off + L], scalar=vcol(wname, wbase + k),
                    in1=rstd[:, 0:L], op0=ALU.mult, op1=ALU.mult),
                    reads=[("x", si, k), "rstd", "vecs"] + (["sqbA", "sqbB"] if out_t is xT else []), writes=[out_key(si, k)])

    def add_into_x(si, m, off, L, pi):
        P.op("dve", lambda e: e.tensor_tensor(out=xT[:, m, off:off + L], in0=xT[:, m, off:off + L], in1=ps[pi][:, 0:L],
                                              op=ALU.add), reads=[("x", si, m), ("ps", pi)], writes=[("x", si, m)])

    def out_proj(segs, w2d, mixkeys=None):
        if mixkeys is None:
            mixkeys = [[("mix", si)] for si in range(len(segs))]
        for n0 in range(0, D, 512):
            wt, wk = wload(lambda t, n0=n0: [(wview(t, KC, 512), rows(w2d)[:, :, n0:n0 + 512])])
            wv = wview(wt, KC, 512)
            for m in range(4):
                for si, sg in enumerate(segs):
                    off, L = sg["off"], sg["L"]
                    pi = newps()
                    mm_group(pi, L, KC, lambda k, m=m, wv=wv: wv[:, k, m * 128:(m + 1) * 128],
                             lambda k, off=off, L=L: mix[:, k, off:off + L], reads=[wk] + mixkeys[si])
                    add_into_x(si, n0 // 128 + m, off, L, pi)

    def ffn(segs, l):
        m0 = A.mark()
        hT = [A.alloc("hT", [128, 2, TT], BF16) for _ in range(2)]
        stmp = [A.alloc("stmp", [128, 512], F32) for _ in range(2)]
        sn = 0
        for blk in range(FC // 2):
            c0 = blk * 256
            wt, wk = wload(lambda t, c0=c0: [(wview(t, KC, 512)[:, :, 0:256], rows(w_up[l])[:, :, c0:c0 + 256]),
                                              (wview(t, KC, 512)[:, :, 256:512], rows(w_up[l])[:, :, DFF + c0:DFF + c0 + 256])])
            wv = wview(wt, KC, 512)
            hb = blk % 2
            for j in range(2):
                for si, sg in enumerate(segs):
                    off, L = sg["off"], sg["L"]
                    pg, pu = newps(), newps()
                    mm_group(pg, L, KC, lambda k, j=j, wv=wv: wv[:, k, j * 128:(j + 1) * 128],
                             lambda k, off=off, L=L: hn[:, k, off:off + L], reads=[wk, ("hn", si)])
                    mm_group(pu, L, KC, lambda k, j=j, wv=wv: wv[:, k, 256 + j * 128:256 + (j + 1) * 128],
                             lambda k, off=off, L=L: hn[:, k, off:off + L], reads=[wk, ("hn", si)])
                    tb = sn % 2
                    sn += 1
                    P.op("act", lambda e, tb=tb, pg=pg, L=L: e.activation(out=stmp[tb][:, 0:L], in_=ps[pg][:, 0:L], func=AF.Silu),
                         reads=[("ps", pg)], writes=[("stmp", tb)])
                    P.op("dve", lambda e, tb=tb, pu=pu, L=L, off=off, hb=hb, j=j: e.tensor_tensor(
                        out=hT[hb][:, j, off:off + L], in0=stmp[tb][:, 0:L], in1=ps[pu][:, 0:L], op=ALU.mult),
                        reads=[("stmp", tb), ("ps", pu)], writes=[("hT", hb, si)])
            wt2, wk2 = wload(lambda t, c0=c0: [(wview(t, 2, D), w_dn[l][c0:c0 + 256, :].rearrange("(j p) n -> p j n", p=128))])
            wv2 = wview(wt2, 2, D)
            for m in range(KC):
                for si, sg in enumerate(segs):
                    off, L = sg["off"], sg["L"]
                    pi = newps()
                    mm_group(pi, L, 2, lambda j, m=m, wv2=wv2: wv2[:, j, m * 128:(m + 1) * 128],
                             lambda j, off=off, L=L, hb=hb: hT[hb][:, j, off:off + L], reads=[wk2, ("hT", hb, si)])
                    add_into_x(si, m, off, L, pi)
        P.barrier()
        A.release(m0)

    def rglru(segs, j):
        m0 = A.mark()
        NS = len(segs)
        nslot = 2 * NS
        sL = [segs[r % NS]["L"] for r in range(nslot)]
        ubuf = [A.alloc("ubuf", [128, 2, sL[r] + 4], F32) for r in range(nslot)]
        uc = [A.alloc("uc", [128, 2, sL[r]], F32) for r in range(nslot)]
        ucb = [A.alloc("ucb", [128, 2, sL[r]], BF16) for r in range(nslot)]
        ggb = [A.alloc("ggb", [128, 2, sL[r]], BF16) for r in range(nslot)]
        TB = [[[A.alloc("rgTB", [128, segs[si]["L"]], F32) for _ in range(2)] for jo in range(2)] for si in range(NS)]

        def loadA(b):
            c0 = b * 256
            wt, wk = wload(lambda t, c0=c0: [(wview(t, KC, 512)[:, :, 0:256], rows(w_rg_in[j])[:, :, c0:c0 + 256]),
                                              (wview(t, KC, 512)[:, :, 256:512], rows(w_rg_in[j])[:, :, D + c0:D + c0 + 256])])
            return wview(wt, KC, 512), wk

        def stageA(b, si, wv, wk):
            lists = []
            sg = segs[si]
            if True:
                off, L, Lr, seq = sg["off"], sg["L"], sg["Lr"], sg["seq"]
                r = (b % 2) * NS + si
                kub, kuc, kucb, kgg = ("ubuf", m0, r), ("uc", m0, r), ("ucb", m0, r), ("ggb", m0, r)
                for jj in range(2):
                    P.record()
                    c = 2 * b + jj
                    sidx = (seq * NRG + j) * KC + c
                    pg, pu = 2 * jj, 2 * jj + 1
                    mm_group(pg, L, KC, lambda k, jj=jj, wv=wv: wv[:, k, jj * 128:(jj + 1) * 128],
                             lambda k, off=off, L=L: hn[:, k, off:off + L], reads=[wk, ("hn", si)])
                    mm_group(pu, L, KC, lambda k, jj=jj, wv=wv: wv[:, k, 256 + jj * 128:256 + (jj + 1) * 128],
                             lambda k, off=off, L=L: hn[:, k, off:off + L], reads=[wk, ("hn", si)])
                    P.op("act", lambda e, r=r, jj=jj, pg=pg, L=L: e.activation(out=ggb[r][:, jj, 0:L], in_=ps[pg][:, 0:L],
                                                                                func=AF.Gelu_apprx_tanh),
                         reads=[("ps", pg)], writes=[(kgg, jj)])
                    P.op("act", lambda e, r=r, jj=jj, pu=pu, L=L: e.activation(out=ubuf[r][:, jj, 3:3 + L], in_=ps[pu][:, 0:L],
                                                                                func=AF.Copy),
                         reads=[("ps", pu)], writes=[(kub, jj, "b")])
                    P.op("dve", lambda e, r=r, jj=jj, sidx=sidx: e.tensor_copy(out=ubuf[r][:, jj, 0:3],
                                                                             in_=cst[:, 3 * sidx:3 * sidx + 3]),
                         reads=[("cst", sidx)], writes=[(kub, jj, "a")])
                    ur = [(kub, jj, "a"), (kub, jj, "b")]
                    cwi = lambda kk, c=c: vcol("cw", (j * 4 + kk) * KC + c)
                    P.op("dve", lambda e, r=r, jj=jj, L=L, c=c, cwi=cwi: e.tensor_scalar(
                        out=uc[r][:, jj, 0:L], in0=ubuf[r][:, jj, 0:L], scalar1=cwi(0), scalar2=vcol("cb", j * KC + c),
                        op0=ALU.mult, op1=ALU.add), reads=ur + ["vecs"], writes=[(kuc, jj)])
                    for kk in range(1, 4):
                        P.op("dve", lambda e, r=r, jj=jj, L=L, kk=kk, cwi=cwi: e.scalar_tensor_tensor(
                            out=uc[r][:, jj, 0:L], in0=ubuf[r][:, jj, kk:kk + L], scalar=cwi(kk), in1=uc[r][:, jj, 0:L],
                            op0=ALU.mult, op1=ALU.add), reads=ur + [(kuc, jj), "vecs"], writes=[(kuc, jj)])
                    P.op("dve", lambda e, r=r, jj=jj, Lr=Lr, sidx=sidx: e.tensor_copy(out=cst[:, 3 * sidx:3 * sidx + 3],
                                                                                     in_=ubuf[r][:, jj, Lr:Lr + 3]),
                         reads=ur, writes=[("cst", sidx)])
                    P.op("act", lambda e, r=r, jj=jj, L=L: e.activation(out=ucb[r][:, jj, 0:L], in_=uc[r][:, jj, 0:L], func=AF.Copy),
                         reads=[(kuc, jj)], writes=[(kucb, jj)])
                    lists.append(P.stop())
            return lists

        def loadB(b):
            wt2, wk2 = wload(lambda t, b=b: [(wview(t, 2, 256), w_rg_a[j, b].rearrange("(c p) n -> p c n", p=128)),
                                              (t[:, 512:1024].rearrange("p (c n) -> p c n", n=256),
                                               w_rg_x[j, b].rearrange("(c p) n -> p c n", p=128))])
            wa = wview(wt2, 2, 256)
            wx = wt2[:, 512:1024].rearrange("p (c n) -> p c n", n=256)
            return wa, wx, wk2

        def stageB(b, si, wa, wx, wk2):
            lists = []
            sg = segs[si]
            if True:
                off, L, Lr, seq = sg["off"], sg["L"], sg["Lr"], sg["seq"]
                r = (b % 2) * NS + si
                kub, kuc, kucb, kgg = ("ubuf", m0, r), ("uc", m0, r), ("ucb", m0, r), ("ggb", m0, r)
                for jo in range(2):
                    P.record()
                    c = 2 * b + jo
                    sidx = (seq * NRG + j) * KC + c
                    pa, px = 4 + 2 * jo, 5 + 2 * jo
                    mm_group(pa, L, 2, lambda ji, jo=jo, wa=wa: wa[:, ji, jo * 128:(jo + 1) * 128],
                             lambda ji, r=r, L=L: ucb[r][:, ji, 0:L], reads=[wk2, (kucb, 0), (kucb, 1)])
                    mm_group(px, L, 2, lambda ji, jo=jo, wx=wx: wx[:, ji, jo * 128:(jo + 1) * 128],
                             lambda ji, r=r, L=L: ucb[r][:, ji, 0:L], reads=[wk2, (kucb, 0), (kucb, 1)])
                    t1, t4 = TB[si][jo]
                    t2 = ubuf[r][:, jo, :]
                    k1, k4 = [("rgTB", m0, si, jo, q) for q in range(2)]
                    k2a, k2b = (kub, jo, "a"), (kub, jo, "b")
                    P.op("act", lambda e, t1=t1, pa=pa, L=L, c=c: e.activation(out=t1[:, 0:L], in_=ps[pa][:, 0:L], func=AF.Tanh, scale=0.5,
                                                                             bias=hb[:, j * KC + c:j * KC + c + 1]),
                         reads=[("ps", pa), "hb"], writes=[k1])
                    P.op("act", lambda e, t4=t4, px=px, L=L, c=c: e.activation(out=t4[:, 0:L], in_=ps[px][:, 0:L], func=AF.Tanh, scale=0.5,
                                                                             bias=hb[:, n1 + j * KC + c:n1 + j * KC + c + 1]),
                         reads=[("ps", px), "hb"], writes=[k4])
                    P.op("act", lambda e, t1=t1, t2=t2, L=L, c=c: e.activation(out=t2[:, 0:L], in_=t1[:, 0:L], func=AF.Exp,
                                                                             scale=clh[:, j * KC + c:j * KC + c + 1],
                                                                             bias=clh[:, j * KC + c:j * KC + c + 1]),
                         reads=[k1, "clh"], writes=[k2a, k2b])
                    P.op("act", lambda e, t1=t1, L=L, c=c: e.activation(out=t1[:, 0:L], in_=t1[:, 0:L], func=AF.Exp,
                                                                      scale=cl[:, j * KC + c:j * KC + c + 1],
                                                                      bias=cl[:, j * KC + c:j * KC + c + 1]),
                         reads=[k1, "cl"], writes=[k1])
                    P.op("act", lambda e, t1=t1, L=L: e.activation(out=t1[:, 0:L], in_=t1[:, 0:L], func=AF.Sqrt, scale=-0.25, bias=0.25),
                         reads=[k1], writes=[k1])
                    P.op("dve", lambda e, t4=t4, r=r, jo=jo, L=L: e.scalar_tensor_tensor(out=t4[:, 0:L], in0=t4[:, 0:L], scalar=1.0,
                                                                                         in1=uc[r][:, jo, 0:L], op0=ALU.add, op1=ALU.mult),
                         reads=[k4, (kuc, jo)], writes=[k4])
                    P.op("dve", lambda e, t4=t4, t1=t1, L=L: e.tensor_tensor(out=t4[:, 0:L], in0=t4[:, 0:L], in1=t1[:, 0:L], op=ALU.mult),
                         reads=[k4, k1], writes=[k4])
                    P.op("dve", lambda e, t2=t2, t4=t4, t1=t1, L=L, sidx=sidx: e.tensor_tensor_scan(
                        out=t1[:, 0:L], data0=t2[:, 0:L], data1=t4[:, 0:L], initial=hst[:, sidx:sidx + 1], op0=ALU.mult, op1=ALU.add),
                        reads=[k2a, k2b, k4, ("hst", sidx)], writes=[k1])
                    P.op("dve", lambda e, t1=t1, Lr=Lr, sidx=sidx: e.tensor_copy(out=hst[:, sidx:sidx + 1], in_=t1[:, Lr - 1:Lr]),
                         reads=[k1], writes=[("hst", sidx)])
                    P.op("dve", lambda e, t1=t1, r=r, jo=jo, L=L, off=off, c=c: e.tensor_tensor(
                        out=mix[:, c, off:off + L], in0=t1[:, 0:L], in1=ggb[r][:, jo, 0:L], op=ALU.mult),
                        reads=[k1, (kgg, jo)], writes=[("mix", si, c)])
                    lists.append(P.stop())
            return lists

        for b in range(cfg.RGB + 1):
            if b < cfg.RGB:
                wA = loadA(b)
            if b > 0:
                wB = loadB(b - 1)
            rest = []
            for si in range(NS):
                lists = []
                if b < cfg.RGB:
                    lists += stageA(b, si, *wA)
                if b > 0:
                    lists += stageB(b - 1, si, *wB)
                P.interleave([l[:4] for l in lists])
                rest += [l[4:] for l in lists]
            P.interleave(rest)
        out_proj(segs, w_rg_out[j], [[("mix", si, c) for c in range(KC)] for si in range(len(segs))])
        P.barrier()
        A.release(m0)

    def gelu_from_psum(pi, L, out_ap, out_key, ta, ka):
        P.op("act", lambda e: e.activation(out=ta[:, 0:L], in_=ps[pi][:, 0:L], func=AF.Square), reads=[("ps", pi)], writes=[ka])
        P.op("dve", lambda e: e.tensor_scalar(out=ta[:, 0:L], in0=ta[:, 0:L], scalar1=0.044715 * 1.5957691216057308,
                                              scalar2=1.5957691216057308, op0=ALU.mult, op1=ALU.add), reads=[ka], writes=[ka])
        P.op("dve", lambda e: e.tensor_tensor(out=ta[:, 0:L], in0=ta[:, 0:L], in1=ps[pi][:, 0:L], op=ALU.mult),
             reads=[ka, ("ps", pi)], writes=[ka])
        P.op("act", lambda e: e.activation(out=ta[:, 0:L], in_=ta[:, 0:L], func=AF.Sigmoid), reads=[ka], writes=[ka])
        P.op("dve", lambda e: e.tensor_tensor(out=out_ap, in0=ta[:, 0:L], in1=ps[pi][:, 0:L], op=ALU.mult),
             reads=[ka, ("ps", pi)], writes=[out_key])

    def gla(segs, j, pidx):
        m0 = A.mark()
        GT = [[[A.alloc("glT", [128, sg["L"]], F32) for _ in range(3)] for dc in range(2)] for sg in segs]
        nblk = [sg["L"] // 128 for sg in segs]
        boff = [sum(nblk[:i]) for i in range(len(segs))]
        NB = sum(nblk)
        NCH = 2 * NB
        glr = A.alloc("glr", [16, TT], BF16)
        QE = [A.alloc("qeT", [128, 2, TT], BF16) for _ in range(2)]
        KE = [A.alloc("keT", [128, 2, TT], BF16) for _ in range(2)]
        KD = [A.alloc("kdT", [128, 2, TT], BF16) for _ in range(2)]
        kdm = [A.alloc("kdm", [128, NB, 256], BF16) for _ in range(2)]
        vsb = A.alloc("vsb", [128, NB, 512], BF16)
        sgT = A.alloc("sgT", [128, 4, TT], BF16)
        DEC = [A.alloc("dec", [128, 2, NCH], F32) for _ in range(2)]
        Sf = [A.alloc("Sf", [128, 2, 512], F32) for _ in range(2)]
        Sb = [A.alloc("Sb", [128, 2, 512], BF16) for _ in range(3)]
        att = [A.alloc("att", [128, 128], BF16) for _ in range(2)]
        sq = A.alloc("sq", [128, 4, 128], BF16)
        rs2 = A.alloc("rs2", [128, 128], F32)
        rstd2 = A.alloc("rstd2", [128, 128], F32)
        otmp = A.alloc("otmp", [128, 4, 128], F32)
        K = lambda *a: ("gla", m0) + a

        wt, wk = wload(lambda t: [(wview(t, KC, 16), rows(w_gla_in[j])[:, :, 2 * DQ + 2 * D:2 * DQ + 2 * D + 16])])
        wv = wview(wt, KC, 16)
        for si, sg in enumerate(segs):
            off, L = sg["off"], sg["L"]
            pi = newps()

            def f(e, pi=pi, off=off, L=L, wv=wv):
                ins = None
                for k in range(KC):
                    ins = e.matmul(ps[pi][0:16, 0:L], lhsT=wv[:, k, 0:16], rhs=hn[:, k, off:off + L], start=(k == 0), stop=(k == KC - 1))
                return ins
            P.op("pe", f, reads=[wk, ("hn", si)], writes=[("ps", pi)])
            P.op("act", lambda e, pi=pi, off=off, L=L: e.activation(out=glr[0:16, off:off + L], in_=ps[pi][0:16, 0:L], func=AF.Copy),
                 reads=[("ps", pi)], writes=[K("glr", si)])
        P.dma("pool", [(wgk[:, :], w_gla_gk2[j])], reads=[], writes=["wgk"], semname="Wgk")

        sver = [0]
        pending = [None]

        def stage2(po, bo, si, h):
            P.op("act", lambda e: e.activation(out=sq[:, :, :], in_=ps[po][:, :].rearrange("p (c t) -> p c t", t=128), func=AF.Square),
                 reads=[("ps", po)], writes=[K("sq")])
            pn = newps()
            mm_group(pn, 128, 4, lambda vc: ones[:, :], lambda vc: sq[:, vc, :], reads=[K("sq"), "ones"])
            P.op("act", lambda e: e.activation(out=rs2[:, :], in_=ps[pn][:, 0:128], func=AF.Ln, bias=EPS, scale=1.0 / 512),
                 reads=[("ps", pn)], writes=[K("rs2")])
            P.op("act", lambda e: e.activation(out=rstd2[:, :], in_=rs2[:, :], func=AF.Exp, scale=-0.5), reads=[K("rs2")], writes=[K("rstd2")])
            P.op("dve", lambda e: e.tensor_tensor(
                out=otmp[:, :, :], in0=ps[po][:, :].rearrange("p (c t) -> p c t", t=128),
                in1=rstd2[:, :].rearrange("p (o t) -> p o t", o=1).broadcast_to([128, 4, 128]), op=ALU.mult),
                reads=[("ps", po), K("rstd2")], writes=[K("otmp")])
            P.op("dve", lambda e: e.tensor_tensor(out=mix[:, h * 4:(h + 1) * 4, bo:bo + 128], in0=otmp[:, :, :], in1=sgT[:, :, bo:bo + 128],
                                                  op=ALU.mult),
                 reads=[K("otmp")] + [K("sg", si, vc) for vc in range(4)], writes=[("mix", si)])

        def pre_load(h):
            wt, wk = wload(lambda t, h=h: [(wview(t, KC, 512)[:, :, 0:256], rows(w_gla_in[j])[:, :, h * 256:(h + 1) * 256]),
                                            (wview(t, KC, 512)[:, :, 256:512], rows(w_gla_in[j])[:, :, DQ + h * 256:DQ + (h + 1) * 256])])
            return wview(wt, KC, 512), wk

        def pre_seg(h, si, wv, wk):
            hp = h % 2
            qeT, keT, kdT, dec = QE[hp], KE[hp], KD[hp], DEC[hp]
            Kq = lambda name, *a_: K(name, hp, *a_)
            sg = segs[si]
            if True:
                off, L = sg["off"], sg["L"]
                nch = L // 64
                ch0 = 2 * boff[si]
                lists = []
                for dc in range(2):
                    P.record()
                    qc = h * 2 + dc
                    pg = 3 * dc
                    P.op("pe", lambda e, pg=pg, qc=qc, off=off, L=L: e.matmul(ps[pg][:, 0:L], lhsT=wgk[0:16, qc * 128:(qc + 1) * 128],
                                                                                rhs=glr[0:16, off:off + L], start=True, stop=True),
                         reads=["wgk", K("glr", si)], writes=[("ps", pg)])
                    t1, t2, t3 = GT[si][dc]
                    k1, k2, k3 = [("glT", m0, si, dc, q_) for q_ in range(3)]
                    P.op("act", lambda e, t1=t1, pg=pg, L=L, qc=qc: e.activation(out=t1[:, 0:L], in_=ps[pg][:, 0:L], func=AF.Exp, scale=-1.0,
                                                                               bias=nbgk[:, j * QC + qc:j * QC + qc + 1]),
                         reads=[("ps", pg), "nbgk"], writes=[k1])
                    P.op("act", lambda e, t1=t1, L=L: e.activation(out=t1[:, 0:L], in_=t1[:, 0:L], func=AF.Ln, bias=1.0), reads=[k1], writes=[k1])
                    P.op("dve", lambda e, t1=t1, t3=t3, L=L: e.tensor_tensor_scan(out=t3[:, 0:L], data0=rmask[:, 0:L], data1=t1[:, 0:L], initial=0.0,
                                                                                  op0=ALU.mult, op1=ALU.add), reads=[k1, "rmask"], writes=[k3])
                    P.op("act", lambda e, t3=t3, L=L, dc=dc, ch0=ch0, nch=nch: e.activation(
                        out=dec[:, dc, ch0:ch0 + nch], in_=t3[:, 0:L].rearrange("p (n c) -> p n c", c=64)[:, :, 63], func=AF.Exp, scale=-1.0 / 16),
                        reads=[k3], writes=[Kq("dec", si, dc)])
                    pq = 3 * dc + 1
                    mm_group(pq, L, KC, lambda k, dc=dc, wv=wv: wv[:, k, dc * 128:(dc + 1) * 128],
                             lambda k, off=off, L=L: hn[:, k, off:off + L], reads=[wk, ("hn", si)])
                    P.op("act", lambda e, t2=t2, t3=t3, L=L: e.activation(out=t2[:, 0:L], in_=t3[:, 0:L], func=AF.Exp, scale=-1.0 / 16),
                         reads=[k3], writes=[k2])
                    P.op("dve", lambda e, t2=t2, pq=pq, L=L, off=off, dc=dc: e.scalar_tensor_tensor(
                        out=qeT[:, dc, off:off + L], in0=ps[pq][:, 0:L], scalar=1.0 / 16.0, in1=t2[:, 0:L], op0=ALU.mult, op1=ALU.mult),
                        reads=[("ps", pq), k2], writes=[Kq("qeT", si, dc)])
                    pk = 3 * dc + 2
                    mm_group(pk, L, KC, lambda k, dc=dc, wv=wv: wv[:, k, 256 + dc * 128:256 + (dc + 1) * 128],
                             lambda k, off=off, L=L: hn[:, k, off:off + L], reads=[wk, ("hn", si)])
                    P.op("act", lambda e, t2=t2, t3=t3, L=L: e.activation(out=t2[:, 0:L], in_=t3[:, 0:L], func=AF.Exp, scale=1.0 / 16),
                         reads=[k3, Kq("qeT", si, dc)], writes=[k2])
                    P.op("dve", lambda e, t2=t2, pk=pk, L=L, off=off, dc=dc: e.tensor_tensor(
                        out=keT[:, dc, off:off + L], in0=ps[pk][:, 0:L], in1=t2[:, 0:L], op=ALU.mult),
                        reads=[("ps", pk), k2], writes=[Kq("keT", si, dc)])
                    P.op("dve", lambda e, t1=t1, t3=t3, L=L, nch=nch: e.tensor_tensor(
                        out=t1[:, 0:L].rearrange("p (n c) -> p n c", c=64), in0=t3[:, 0:L].rearrange("p (n c) -> p n c", c=64),
                        in1=t3[:, 0:L].rearrange("p (n c) -> p n c", c=64)[:, :, 63:64].broadcast_to([128, nch, 64]), op=ALU.subtract),
                        reads=[k3], writes=[k1])
                    P.op("act", lambda e, t1=t1, L=L: e.activation(out=t1[:, 0:L], in_=t1[:, 0:L], func=AF.Exp, scale=1.0 / 16), reads=[k1], writes=[k1])
                    P.op("dve", lambda e, t1=t1, pk=pk, L=L, off=off, dc=dc: e.tensor_tensor(
                        out=kdT[:, dc, off:off + L], in0=ps[pk][:, 0:L], in1=t1[:, 0:L], op=ALU.mult),
                        reads=[("ps", pk), k1], writes=[Kq("kdT", si, dc)])
                    lists.append(P.stop())
                P.interleave(lists)
        def head_main(h):
            hp = h % 2
            qeT, keT, kdT, dec = QE[hp], KE[hp], KD[hp], DEC[hp]
            Kq = lambda name, *a_: K(name, hp, *a_)
            wt, wk = wload(lambda t, h=h: [(wview(t, KC, 512), rows(w_gla_in[j])[:, :, 2 * DQ + h * 512:2 * DQ + (h + 1) * 512])])
            wv = wview(wt, KC, 512)
            for si, sg in enumerate(segs):
                off = sg["off"]
                for b in range(nblk[si]):
                    bo = off + b * 128
                    bi = boff[si] + b
                    pv = newps()
                    mm_group(pv, 512, KC, lambda k, bo=bo: hn[:, k, bo:bo + 128], lambda k, wv=wv: wv[:, k, :], reads=[wk, ("hn", si)])
                    P.op("act", lambda e, pv=pv, bi=bi: e.activation(out=vsb[:, bi, :], in_=ps[pv][:, :], func=AF.Copy),
                         reads=[("ps", pv)], writes=[K("v", bi)])
            wt, wk = wload(lambda t, h=h: [(wview(t, KC, 512), rows(w_gla_in[j])[:, :, 2 * DQ + D + h * 512:2 * DQ + D + (h + 1) * 512])])
            wv = wview(wt, KC, 512)
            wg_v, wg_k = wv, wk
            fillers = {si: [] for si in range(len(segs))}

            def g_group(vc, si, wv=wg_v, wk=wg_k):
                sg = segs[si]
                off, L = sg["off"], sg["L"]
                pg = newps()
                mm_group(pg, L, KC, lambda k: wv[:, k, vc * 128:(vc + 1) * 128],
                         lambda k: hn[:, k, off:off + L], reads=[wk, ("hn", si)])
                P.op("act", lambda e: e.activation(out=sgT[:, vc, off:off + L], in_=ps[pg][:, 0:L], func=AF.Silu),
                     reads=[("ps", pg)], writes=[K("sg", si, vc)])
                P.op("dve", lambda e: e.tensor_scalar(out=sgT[:, vc, off:off + L], in0=sgT[:, vc, off:off + L], scalar1=vcol("gnw", j * 4 + vc),
                                                      scalar2=None, op0=ALU.mult), reads=[K("sg", si, vc), "vecs"], writes=[K("sg", si, vc)])
            forder = []
            for seq_ in (0, 1):
                for si, sg in enumerate(segs):
                    if sg["seq"] == seq_:
                        for vc in range(4):
                            forder.append(("g", si, vc))
                        if h + 1 < H:
                            forder.append(("pre", si, None))
            fpos = [0]
            nxt = [None]

            def fill(n=None, upto_si=None):
                while fpos[0] < len(forder):
                    if n is not None and n <= 0:
                        break
                    if upto_si is not None and not any(it[0] == "g" and it[1] == upto_si for it in forder[fpos[0]:]):
                        break
                    kind, s, vc = forder[fpos[0]]
                    fpos[0] += 1
                    if kind == "g":
                        g_group(vc, s)
                    else:
                        if nxt[0] is None:
                            nxt[0] = pre_load(h + 1)
                        pre_seg(h + 1, s, *nxt[0])
                    if n is not None:
                        n -= 1
            for si, sg in enumerate(segs):
                off = sg["off"]
                for b in range(nblk[si]):
                    bo = off + b * 128
                    bi = boff[si] + b
                    pt = newps()

                    def f(e, pt=pt, bo=bo):
                        ins = None
                        for dc in range(2):
                            ins = e.transpose(out=psb[pt][:, dc * 128:(dc + 1) * 128], in_=kdT[:, dc, bo:bo + 128], identity=ident[:, :])
                        return ins
                    P.op("pe", f, reads=[Kq("kdT", si, 0), Kq("kdT", si, 1), "ident"], writes=[("ps", pt)])
                    P.op("dve", lambda e, pt=pt, bi=bi: e.tensor_scalar(out=kdm[0][:, bi, :], in0=psb[pt][:, 0:256], scalar1=rmA[:, 0:1],
                                                                         scalar2=None, op0=ALU.mult), reads=[("ps", pt), "rmA"], writes=[K("kdmA", bi)])
                    P.op("dve", lambda e, pt=pt, bi=bi: e.tensor_scalar(out=kdm[1][:, bi, :], in0=psb[pt][:, 0:256], scalar1=rmB[:, 0:1],
                                                                         scalar2=None, op0=ALU.mult), reads=[("ps", pt), "rmB"], writes=[K("kdmB", bi)])
            for seq in (0, 1):
                ssegs = [(si, sg) for si, sg in enumerate(segs) if sg["seq"] == seq]
                if not ssegs:
                    continue
                S = Sf[seq]
                kS = K("Sf", seq)
                dkey = ("dS", j, h, seq)
                Sview = S[:, :, :]
                kS2 = [(kS, 0), (kS, 1)]
                if seq == 1:
                    P.dma("sp", [(Sview, sgla_d[j, h].rearrange("(c p) v -> p c v", p=128))], reads=[], writes=kS2, semname=f"LS{seq}")
                elif pidx == 0:
                    P.op("dve", lambda e, S=S: e.memset(S[:, :, :], 0.0), writes=kS2)
                else:
                    P.dma("sp", [(Sview, o_gla[0][j, h].rearrange("(c p) v -> p c v", p=128))], reads=[dkey], writes=kS2, semname=f"LS{seq}")
                v0 = sver[0]
                P.op("act", lambda e, S=S, v0=v0: e.activation(out=Sb[v0 % 3][:, :, :], in_=S[:, :, :], func=AF.Copy),
                     reads=kS2, writes=[K("Sb", v0 % 3, 0), K("Sb", v0 % 3, 1)])
                for si, sg in ssegs:
                    off, L, Lr = sg["off"], sg["L"], sg["Lr"]
                    for b in range(nblk[si]):
                        bo = off + b * 128
                        bi = boff[si] + b
                        vers = []
                        for half in range(2):
                            vers.append(sver[0])
                            if b * 128 + half * 64 >= Lr:
                                continue
                            ch = 2 * bi + half
                            vn = sver[0] + 1
                            for dc in range(2):
                                pd = newps()
                                P.op("pe", lambda e, pd=pd, half=half, bi=bi, dc=dc: e.matmul(
                                    ps[pd][:, :], lhsT=kdm[half][:, bi, dc * 128:(dc + 1) * 128], rhs=vsb[:, bi, :], start=True, stop=True),
                                    reads=[K("kdmA" if half == 0 else "kdmB", bi), K("v", bi)], writes=[("ps", pd)])
                                P.op("dve", lambda e, S=S, dc=dc, ch=ch, pd=pd: e.scalar_tensor_tensor(
                                    out=S[:, dc, :], in0=S[:, dc, :], scalar=dec[:, dc, ch:ch + 1], in1=ps[pd][:, :], op0=ALU.mult, op1=ALU.add),
                                    reads=[(kS, dc), Kq("dec", si, dc), ("ps", pd)], writes=[(kS, dc)])
                                P.op("act", lambda e, S=S, dc=dc, vn=vn: e.activation(out=Sb[vn % 3][:, dc, :], in_=S[:, dc, :], func=AF.Copy),
                                     reads=[(kS, dc)], writes=[K("Sb", vn % 3, dc)])
                            sver[0] = vn
                        pa = newps()
                        mm_group(pa, 128, 2, lambda dc, bo=bo: keT[:, dc, bo:bo + 128], lambda dc, bo=bo: qeT[:, dc, bo:bo + 128],
                                 reads=[Kq("keT", si, 0), Kq("keT", si, 1), Kq("qeT", si, 0), Kq("qeT", si, 1)])
                        ab = bi % 2
                        P.op("dve", lambda e, pa=pa, ab=ab: e.tensor_tensor(out=att[ab][:, :], in0=ps[pa][:, 0:128], in1=mask[:, :], op=ALU.mult),
                             reads=[("ps", pa), "mask"], writes=[K("att", ab)])
                        po = newpo()
                        fill(n=2)

                        def fo(e, po=po, bi=bi, bo=bo, ab=ab, vers=tuple(vers)):
                            ins = None
                            for vc in range(4):
                                ov = ps[po][:, vc * 128:(vc + 1) * 128]
                                e.matmul(ov, lhsT=vsb[:, bi, vc * 128:(vc + 1) * 128], rhs=att[ab][:, :], start=True, stop=False)
                                for half in range(2):
                                    sbv = Sb[vers[half] % 3]
                                    for dc in range(2):
                                        last = (half == 1 and dc == 1)
                                        ins = e.matmul(ov[:, half * 64:(half + 1) * 64], lhsT=sbv[:, dc, vc * 128:(vc + 1) * 128],
                                                       rhs=qeT[:, dc, bo + half * 64:bo + (half + 1) * 64], start=False, stop=last)
                            return ins
                        P.op("pe", fo, reads=[K("v", bi), K("att", ab), K("Sb", vers[0] % 3, 0), K("Sb", vers[0] % 3, 1), K("Sb", vers[1] % 3, 0), K("Sb", vers[1] % 3, 1),
                                    Kq("qeT", si, 0), Kq("qeT", si, 1)],
                             writes=[("ps", po)])
                        if pending[0] is not None:
                            fill(upto_si=pending[0][2])
                            stage2(*pending[0])
                        pending[0] = (po, bo, si, h)
                if pending[0] is not None:
                    fill(upto_si=pending[0][2])
                    stage2(*pending[0])
                    pending[0] = None
                P.dma("sp", [(o_gla[seq][j, h].rearrange("(c p) v -> p c v", p=128), Sview)], reads=[(kS, 0), (kS, 1)], writes=[dkey], semname=f"SS{seq}")
            fill()
        w0 = pre_load(0)
        for si in range(len(segs)):
            pre_seg(0, si, *w0)
        for h in range(H):
            head_main(h)
        out_proj(segs, w_gla_out[j])
        P.barrier()
        A.release(m0)

    for pidx, pss in enumerate(cfg.passes):
        segs = []
        off = 0
        for (seq, t0, L, Lr) in pss:
            segs.append(dict(seq=seq, t0=t0, L=L, Lr=Lr, off=off))
            off += L
        for si, sg in enumerate(segs):
            o_, L, Lr = sg["off"], sg["L"], sg["Lr"]
            xk = [("x", si, m) for m in range(KC)]
            P.dma("sp", [(xT[:, :, o_:o_ + Lr], xin[sg["seq"]][:, sg["t0"]:sg["t0"] + Lr].rearrange("(k p) t -> p k t", p=128))],
                  reads=[], writes=xk, semname=f"Lx{si}")
            if Lr < L:
                P.op("dve", lambda e, o_=o_, L=L, Lr=Lr: e.memset(xT[:, :, o_ + Lr:o_ + L], 0.0), reads=xk, writes=xk)
        for l in range(cfg.DEPTH):
            j = l // 2
            rmsnorm(segs, "nm", l * KC, hn, lambda si, k: ("hn", si), sqb)
            if l % 2 == 0:
                rglru(segs, j)
            else:
                gla(segs, j, pidx)
            rmsnorm(segs, "nf", l * KC, hn, lambda si, k: ("hn", si), sqb)
            ffn(segs, l)
        rmsnorm(segs, "nfin", 0, xT, lambda si, k: ("x", si, k), sqb)
        for si, sg in enumerate(segs):
            o_, Lr = sg["off"], sg["Lr"]
            xk = [("x", si, m) for m in range(KC)]
            P.dma("sp", [(yout[sg["seq"]][:, sg["t0"]:sg["t0"] + Lr].rearrange("(k p) t -> p k t", p=128), xT[:, :, o_:o_ + Lr])],
                  reads=xk, writes=[("y", si)], semname=f"SY{si}")
    P.dma("sp", [(o_h[:, :], hst[:, :])], reads=["hst"] + [("hst", i) for i in range(2 * NRG * KC)], writes=["o_h"], semname="SH")
    P.dma("sp", [(o_conv[:, :], cst[:, :])], reads=["cst"] + [("cst", i) for i in range(2 * NRG * KC)], writes=["o_conv"], semname="SC")
    P.final_wait("sp")
    P.emit()
    cfg.arena_peak = A.peak
    return nc


def _pc(v):
    v = np.asarray(v, np.float32)
    lead = v.shape[:-1]
    C = v.shape[-1] // 128
    a = v.reshape(lead + (C, 128))
    a = np.moveaxis(a, -1, 0)
    return np.ascontiguousarray(a.reshape(128, -1))


def pack_vecs(cfg, inp, b):
    parts = [
        _pc(inp["norm_mix"]), _pc(inp["norm_ffn"]), _pc(inp["norm_final"]),
        _pc(inp["rg_conv_w"]), _pc(inp["rg_conv_b"]), _pc(inp["rg_b_a"]), _pc(inp["rg_b_x"]), _pc(inp["rg_lambda"]),
        _pc(inp["gla_b_gk"]), _pc(inp["gla_norm_w"]),
        _pc(inp["state_rglru_h"][:, b, :]),
        np.ascontiguousarray(np.transpose(np.asarray(inp["state_rglru_conv"][:, b], np.float32).reshape(cfg.NRG, 3, cfg.KC, 128),
                                          (3, 0, 2, 1)).reshape(128, -1)),
    ]
    out = np.concatenate(parts, axis=1)
    assert out.shape[1] == cfg.NV, (out.shape, cfg.NV)
    return out


WNAMES = ["rg_w_in", "rg_w_a", "rg_w_x", "rg_w_out", "gla_w_in", "gla_w_gk2", "gla_w_out", "ffn_w_up", "ffn_w_down"]


def make_in_maps(cfg, inp, ncores):
    maps = []
    w = {k: np.ascontiguousarray(np.asarray(inp[k], np.float32)) for k in WNAMES}
    for b in range(ncores):
        m = dict(w)
        m["xp"] = np.ascontiguousarray(np.asarray(inp["x_prompt"][b], np.float32).T)
        m["xs"] = np.ascontiguousarray(np.asarray(inp["x_sample"][b], np.float32).T)
        m["vecs"] = pack_vecs(cfg, inp, b)
        m["sgla"] = np.ascontiguousarray(np.asarray(inp["state_gla"][:, b], np.float32))
        maps.append(m)
    return maps


def gather(cfg, results):
    nb = len(results)
    KC, NRG = cfg.KC, cfg.NRG
    y_p = np.stack([r["yp"].T for r in results]).astype(np.float32)
    y_s = np.stack([r["ys"].T for r in results]).astype(np.float32)
    h = np.stack([r["o_h"].reshape(128, 2, NRG, KC) for r in results])
    h = np.transpose(h, (2, 3, 0, 4, 1)).reshape(2, NRG, nb, KC * 128)
    c = np.stack([r["o_conv"].reshape(128, 2, NRG, KC, 3) for r in results])
    c = np.transpose(c, (2, 3, 0, 5, 4, 1)).reshape(2, NRG, nb, 3, KC * 128)
    gp = np.stack([r["o_gla_p"] for r in results], axis=1)
    gs = np.stack([r["o_gla_s"] for r in results], axis=1)
    f = lambda a: np.ascontiguousarray(a, dtype=np.float32)
    return (f(y_p), f(y_s), f(h[0]), f(c[0]), f(gp), f(h[1]), f(c[1]), f(gs))


def kernel(**inputs):
    cfg = Cfg()
    nc = build(cfg)
    maps = make_in_maps(cfg, inputs, 8)
    res = run_bass_kernel_spmd(nc, maps, core_ids=list(range(8)))
    return gather(cfg, res.results)
```
